# Optimizing a Trainium2 kernel written in Bass

```python
import math
import jax
import jax.numpy as jnp
from jax import lax
import numpy as np

D_MODEL = 1024
BATCH = 8
SEQ = 4096
DEPTH = 2

HEAD_DIM = 64
SCALE = HEAD_DIM ** -0.5
NEG_INF = -1e30
EPS = 1e-6
A_GROUPS = ((128, 1), (512, 4), (2048, 16))
A_HEADS_PER_GROUP = 4
A_HEADS = A_HEADS_PER_GROUP * len(A_GROUPS)
A_WIDTH = A_HEADS * HEAD_DIM
A_OUT = A_HEADS_PER_GROUP * HEAD_DIM
A_BLOCK = max(w // d for w, d in A_GROUPS)
LRU_WIDTH = D_MODEL // 2
LRU_BLOCKS = 8
LRU_BLOCK_DIM = LRU_WIDTH // LRU_BLOCKS
CONV_WIDTH = 4
LRU_C = 8.0
C_HEADS = 8
C_WIDTH = C_HEADS * HEAD_DIM
MOBA_BLOCK = 256
MOBA_TOPK = 3
MOBA_Q_CHUNK = 32
REL_BUCKETS = 32
REL_MAX_DIST = 2048
REL_HEADS = A_HEADS + C_HEADS
FFN_HIDDEN = -(-8 * D_MODEL // (3 * 256)) * 256
IN_SIZES = (A_WIDTH, A_WIDTH, A_WIDTH, LRU_WIDTH, LRU_WIDTH, C_WIDTH, C_WIDTH, C_WIDTH, 3 * D_MODEL)
IN_COLS = sum(IN_SIZES)

kernel_name = "hybrid_dilated_rglru_moba_block"


def rms_norm(x, g):
    xf = x.astype(jnp.float32)
    y = xf * lax.rsqrt(jnp.mean(xf * xf, axis=-1, keepdims=True) + EPS)
    return (y * g.astype(jnp.float32)).astype(x.dtype)


def rel_bucket(dist):
    max_exact = REL_BUCKETS // 2
    d = jnp.maximum(dist, 0)
    df = jnp.maximum(d, 1).astype(jnp.float32)
    large = max_exact + (jnp.log(df / max_exact) / math.log(REL_MAX_DIST / max_exact)
                         * (REL_BUCKETS - max_exact)).astype(jnp.int32)
    large = jnp.minimum(large, REL_BUCKETS - 1)
    return jnp.where(d < max_exact, d, large)


def dilated_window_attention(q, k, v, dilation, n_back, bias_tab):
    B, S, H, E = q.shape
    L = S // dilation
    nb = -(-L // A_BLOCK)
    Lp = nb * A_BLOCK

    def to_sub(t):
        t = t.reshape(B, L, dilation, H, E).transpose(0, 2, 3, 1, 4)
        t = jnp.pad(t, ((0, 0), (0, 0), (0, 0), (0, Lp - L), (0, 0)))
        return t.reshape(B, dilation, H, nb, A_BLOCK, E)

    qb, kb, vb = to_sub(q), to_sub(k), to_sub(v)

    def with_prev(t):
        prev = jnp.pad(t, ((0, 0), (0, 0), (0, 0), (1, 0), (0, 0), (0, 0)))[:, :, :, :nb]
        return jnp.concatenate([prev, t], axis=4)

    kk, vv = with_prev(kb), with_prev(vb)
    qi = jnp.arange(A_BLOCK)[:, None] + A_BLOCK
    kj = jnp.arange(2 * A_BLOCK)[None, :]
    delta = qi - kj
    band = (delta >= 0) & (delta <= n_back)
    has_prev = (jnp.arange(nb)[:, None, None] > 0) | (kj >= A_BLOCK)[None]
    valid = band[None] & has_prev
    bias = bias_tab.astype(jnp.float32)[rel_bucket(delta * dilation)]
    bias = jnp.transpose(bias, (2, 0, 1))
    logits = jnp.einsum('bdhnqe,bdhnke->bdhnqk', qb, kk).astype(jnp.float32) * SCALE
    logits = logits + bias[None, None, :, None]
    logits = jnp.where(valid[None, None, None], logits, NEG_INF)
    m = jnp.max(logits, axis=-1, keepdims=True)
    p = jnp.exp(logits - m)
    den = jnp.sum(p, axis=-1, keepdims=True)
    o = jnp.einsum('bdhnqk,bdhnke->bdhnqe', p, vv.astype(jnp.float32)) / den
    lse = (m + jnp.log(den))[..., 0]
    o = o.reshape(B, dilation, H, Lp, E)[:, :, :, :L].transpose(0, 3, 1, 2, 4).reshape(B, S, H, E)
    lse = lse.reshape(B, dilation, H, Lp)[:, :, :, :L].transpose(0, 3, 1, 2).reshape(B, S, H)
    return o, lse


def rg_lru_branch(xb, gb, conv_w, conv_b, w_a, b_a, w_x, b_x, lam):
    B, S, W = xb.shape
    xp = jnp.pad(xb, ((0, 0), (CONV_WIDTH - 1, 0), (0, 0)))
    xc = conv_b
    for i in range(CONV_WIDTH):
        xc = xc + xp[:, i:i + S] * conv_w[i]
    xr = xc.reshape(B, S, LRU_BLOCKS, LRU_BLOCK_DIM)
    r = jax.nn.sigmoid((jnp.einsum('bsnd,nde->bsne', xr, w_a).reshape(B, S, W) + b_a).astype(jnp.float32))
    ig = jax.nn.sigmoid((jnp.einsum('bsnd,nde->bsne', xr, w_x).reshape(B, S, W) + b_x).astype(jnp.float32))
    log_a = -LRU_C * r * jax.nn.softplus(-lam.astype(jnp.float32))
    a = jnp.exp(log_a)
    bterm = jnp.sqrt(-jnp.expm1(2.0 * log_a)) * (ig * xc.astype(jnp.float32))

    def combine(left, right):
        a1, b1 = left
        a2, b2 = right
        return a1 * a2, a2 * b1 + b2

    _, h = lax.associative_scan(combine, (a, bterm), axis=1)
    return (h * jax.nn.gelu(gb.astype(jnp.float32))).astype(xb.dtype)


def moba_attention(q, k, v, bias_tab):
    B, S, H, E = q.shape
    f32 = jnp.float32
    nblk = -(-S // MOBA_BLOCK)
    Sp = nblk * MOBA_BLOCK
    pad = ((0, 0), (0, 0), (0, Sp - S), (0, 0))
    qh = q.transpose(0, 2, 1, 3)
    qb = jnp.pad(qh, pad).reshape(B, H, nblk, MOBA_BLOCK, E)
    kb = jnp.pad(k.transpose(0, 2, 1, 3), pad).reshape(B, H, nblk, MOBA_BLOCK, E)
    vb = jnp.pad(v.transpose(0, 2, 1, 3), pad).reshape(B, H, nblk, MOBA_BLOCK, E)
    tab = bias_tab.astype(f32)
    off = jnp.arange(MOBA_BLOCK)
    delta = off[:, None] - off[None, :]
    bias_own = jnp.transpose(tab[rel_bucket(delta)], (2, 0, 1))
    lg = jnp.einsum('bhnqe,bhnke->bhnqk', qb, kb).astype(f32) * SCALE + bias_own[None, :, None]
    lg = jnp.where(delta >= 0, lg, NEG_INF)
    m_own = jnp.max(lg, axis=-1)
    p = jnp.exp(lg - m_own[..., None])
    den_own = jnp.sum(p, axis=-1).reshape(B, H, Sp)[:, :, :S]
    num_own = jnp.einsum('bhnqk,bhnke->bhnqe', p, vb.astype(f32)).reshape(B, H, Sp, E)[:, :, :S]
    m_own = m_own.reshape(B, H, Sp)[:, :, :S]
    n_sel = min(MOBA_TOPK, nblk - 1)
    if n_sel == 0:
        out = num_own / den_own[..., None]
    else:
        kmean = jnp.mean(kb.astype(f32), axis=3)
        qpos = jnp.arange(S)
        qblk = qpos // MOBA_BLOCK
        gate = jnp.einsum('bhse,bhne->bhsn', qh.astype(f32), kmean)
        gate = jnp.where(jnp.arange(nblk)[None, :] < qblk[:, None], gate, NEG_INF)
        _, idx = lax.top_k(gate, n_sel)
        base = (jnp.arange(B)[:, None] * H + jnp.arange(H)[None, :]) * nblk
        fidx = idx + base[:, :, None, None]
        kflat = kb.reshape(B * H * nblk, MOBA_BLOCK, E)
        vflat = vb.reshape(B * H * nblk, MOBA_BLOCK, E)
        head = jnp.arange(H)[None, :, None, None, None]
        tab_t = tab.T
        nq = S // MOBA_Q_CHUNK

        def chunks(t):
            return jnp.moveaxis(t.reshape(B, H, nq, MOBA_Q_CHUNK, *t.shape[3:]), 2, 0)

        def step(args):
            qc, fi, bi, qp, m_o, d_o, n_o = args
            ks = jnp.take(kflat, fi, axis=0)
            vs = jnp.take(vflat, fi, axis=0)
            lgs = jnp.einsum('bhqe,bhqnke->bhqnk', qc, ks).astype(f32) * SCALE
            kpos = bi[..., None] * MOBA_BLOCK + jnp.arange(MOBA_BLOCK)
            lgs = lgs + tab_t[head, rel_bucket(qp[:, None, None] - kpos)]
            ok = bi < (qp // MOBA_BLOCK)[:, None]
            lgs = jnp.where(ok[..., None], lgs, NEG_INF)
            mm = jnp.maximum(jnp.max(lgs, axis=(-2, -1)), m_o)
            ps = jnp.exp(lgs - mm[..., None, None])
            c_o = jnp.exp(m_o - mm)
            den = d_o * c_o + jnp.sum(ps, axis=(-2, -1))
            num = n_o * c_o[..., None] + jnp.einsum('bhqnk,bhqnke->bhqe', ps, vs.astype(f32))
            return num / den[..., None]

        out = lax.map(step, (chunks(qh), chunks(fidx), chunks(idx), qpos.reshape(nq, MOBA_Q_CHUNK),
                             chunks(m_own), chunks(den_own), chunks(num_own)))
        out = jnp.moveaxis(out, 0, 2).reshape(B, H, S, E)
    return out.transpose(0, 2, 1, 3)


def hybrid_layer(x, rel_bias, g_mix, w_in, conv_w, conv_b, lru_wa, lru_ba, lru_wx, lru_bx, lru_lam,
                 p_a, p_b, p_c, w_out, g_ffn, w_gu, w_down):
    B, S, D = x.shape
    h = rms_norm(x, g_mix)
    z = h @ w_in
    offs = np.cumsum(IN_SIZES)[:-1].tolist()
    qa, ka, va, xb, gb, qc, kc, vc, gates = jnp.split(z, offs, axis=-1)
    qa = qa.reshape(B, S, A_HEADS, HEAD_DIM)
    ka = ka.reshape(B, S, A_HEADS, HEAD_DIM)
    va = va.reshape(B, S, A_HEADS, HEAD_DIM)
    outs, lses = [], []
    for g, (window, dilation) in enumerate(A_GROUPS):
        sl = slice(g * A_HEADS_PER_GROUP, (g + 1) * A_HEADS_PER_GROUP)
        o, lse = dilated_window_attention(qa[:, :, sl], ka[:, :, sl], va[:, :, sl],
                                          dilation, window // dilation, rel_bias[:, sl])
        outs.append(o)
        lses.append(lse)
    wts = jax.nn.softmax(jnp.stack(lses, axis=0), axis=0)
    o_a = jnp.sum(wts[..., None] * jnp.stack(outs, axis=0), axis=0).reshape(B, S, A_OUT).astype(x.dtype)
    o_b = rg_lru_branch(xb, gb, conv_w, conv_b, lru_wa, lru_ba, lru_wx, lru_bx, lru_lam)
    o_c = moba_attention(qc.reshape(B, S, C_HEADS, HEAD_DIM), kc.reshape(B, S, C_HEADS, HEAD_DIM),
                         vc.reshape(B, S, C_HEADS, HEAD_DIM), rel_bias[:, A_HEADS:])
    o_c = o_c.reshape(B, S, C_WIDTH).astype(x.dtype)
    gt = jax.nn.sigmoid(gates.astype(jnp.float32)).astype(x.dtype).reshape(B, S, 3, D)
    merged = gt[:, :, 0] * (o_a @ p_a) + gt[:, :, 1] * (o_b @ p_b) + gt[:, :, 2] * (o_c @ p_c)
    x = x + merged @ w_out
    u = rms_norm(x, g_ffn) @ w_gu
    gate, up = jnp.split(u, 2, axis=-1)
    return x + (jax.nn.silu(gate) * up) @ w_down


def setup_inputs(seed: int = 0) -> dict:
    key = jax.random.key(seed)
    ks = jax.random.split(key, 24)
    f32 = jnp.float32

    def nrm(k, shape, scale):
        return jax.random.normal(k, shape, f32) * scale

    u = jax.random.uniform(ks[11], (DEPTH, LRU_WIDTH), f32, 0.9, 0.999)
    a = u ** (1.0 / LRU_C)
    return {
        "x": nrm(ks[0], (BATCH, SEQ, D_MODEL), 1.0),
        "rel_bias": nrm(ks[1], (REL_BUCKETS, REL_HEADS), 0.5),
        "g_mix": 1.0 + nrm(ks[2], (DEPTH, D_MODEL), 0.05),
        "w_in": nrm(ks[3], (DEPTH, D_MODEL, IN_COLS), D_MODEL ** -0.5),
        "conv_w": nrm(ks[4], (DEPTH, CONV_WIDTH, LRU_WIDTH), CONV_WIDTH ** -0.5),
        "conv_b": nrm(ks[5], (DEPTH, LRU_WIDTH), 0.02),
        "lru_wa": nrm(ks[6], (DEPTH, LRU_BLOCKS, LRU_BLOCK_DIM, LRU_BLOCK_DIM), LRU_BLOCK_DIM ** -0.5),
        "lru_ba": nrm(ks[7], (DEPTH, LRU_WIDTH), 0.02),
        "lru_wx": nrm(ks[8], (DEPTH, LRU_BLOCKS, LRU_BLOCK_DIM, LRU_BLOCK_DIM), LRU_BLOCK_DIM ** -0.5),
        "lru_bx": nrm(ks[9], (DEPTH, LRU_WIDTH), 0.02),
        "lru_lam": jnp.log(a) - jnp.log1p(-a),
        "p_a": nrm(ks[12], (DEPTH, A_OUT, D_MODEL), A_OUT ** -0.5),
        "p_b": nrm(ks[13], (DEPTH, LRU_WIDTH, D_MODEL), LRU_WIDTH ** -0.5),
        "p_c": nrm(ks[14], (DEPTH, C_WIDTH, D_MODEL), C_WIDTH ** -0.5),
        "w_out": nrm(ks[15], (DEPTH, D_MODEL, D_MODEL), D_MODEL ** -0.5),
        "g_ffn": 1.0 + nrm(ks[16], (DEPTH, D_MODEL), 0.05),
        "w_gu": nrm(ks[17], (DEPTH, D_MODEL, 2 * FFN_HIDDEN), D_MODEL ** -0.5),
        "w_down": nrm(ks[18], (DEPTH, FFN_HIDDEN, D_MODEL), FFN_HIDDEN ** -0.5),
        "g_final": 1.0 + nrm(ks[19], (D_MODEL,), 0.05),
    }


def reference(x, rel_bias, g_mix, w_in, conv_w, conv_b, lru_wa, lru_ba, lru_wx, lru_bx, lru_lam,
              p_a, p_b, p_c, w_out, g_ffn, w_gu, w_down, g_final):
    for l in range(DEPTH):
        x = hybrid_layer(x, rel_bias, g_mix[l], w_in[l], conv_w[l], conv_b[l], lru_wa[l], lru_ba[l],
                         lru_wx[l], lru_bx[l], lru_lam[l], p_a[l], p_b[l], p_c[l], w_out[l],
                         g_ffn[l], w_gu[l], w_down[l])
    return rms_norm(x, g_final)
```

```python
import contextlib
import numpy as np
import ml_dtypes

import concourse.bass as bass
import concourse.mybir as mybir
from concourse.bass_utils import run_bass_kernel_spmd

F32 = mybir.dt.float32
BF16 = mybir.dt.bfloat16
AF = mybir.ActivationFunctionType
ALU = mybir.AluOpType
AX = mybir.AxisListType

S_LEN = 4096
D = 1024
NT = 32
NDC = 8
NTC = 8
DEPTH = 2
IN_COLS = 7936
QA, KA, VA, XB, GB, QC, KC, VC, GT = 0, 768, 1536, 2304, 2816, 3328, 3840, 4352, 4864
FFN = 2816
NFC = 22
SCALE = 0.125
EPS = 1e-6
NEGM = -262144.0
NEG_BIAS = -30000.0
STRIP_W = 2560
STRIP_OFFMAX = 2048
SEM_LIMIT = 30000


class Buf:
    __slots__ = ("name", "writers", "readers", "dsem", "dcnt")

    def __init__(self, name):
        self.name = name
        self.writers = {}
        self.readers = {}
        self.dsem = None
        self.dcnt = 0


class Sched:
    def __init__(self, nc):
        self.nc = nc
        self.eng = {"pe": nc.tensor, "act": nc.scalar, "dve": nc.vector, "pool": nc.gpsimd, "sp": nc.sync}
        self.sem = {}
        self.cnt = {}
        self.seen = {e: {} for e in self.eng}
        self.nsem = 0
        self.dbufs = []
        self.free_dsems = []
        for e in self.eng:
            self._new_sem(e)
        self.ninstr = 0

    def _alloc(self, name):
        self.nsem += 1
        return self.nc.alloc_semaphore(f"{name}_{self.nsem}")

    def _new_sem(self, e):
        self.sem[e] = self._alloc("s_" + e)
        self.cnt[e] = 0

    def _wait(self, e, deps):
        seen = self.seen[e]
        own = self.sem[e]
        for s, v in deps.items():
            if s is own and e == "pe":
                continue
            if seen.get(s, 0) >= v:
                continue
            self.eng[e].wait_ge(s, v)
            self.ninstr += 1
            seen[s] = v

    @staticmethod
    def _collect(reads, writes, waw):
        deps = {}
        for b in reads:
            for s, v in b.writers.items():
                if deps.get(s, 0) < v:
                    deps[s] = v
        for b in writes:
            if waw:
                for s, v in b.writers.items():
                    if deps.get(s, 0) < v:
                        deps[s] = v
            for s, v in b.readers.items():
                if deps.get(s, 0) < v:
                    deps[s] = v
        return deps

    @staticmethod
    def _record(s, v, reads, writes, waw):
        for b in reads:
            if b.readers.get(s, 0) < v:
                b.readers[s] = v
        for b in writes:
            if waw:
                b.writers = {s: v}
            else:
                b.writers[s] = v
            b.readers = {}

    def op(self, e, fn, reads=(), writes=(), waw=True):
        self._wait(e, self._collect(reads, writes, waw))
        if self.cnt[e] >= SEM_LIMIT:
            self._new_sem(e)
        ins = fn()
        self.cnt[e] += 1
        s, v = self.sem[e], self.cnt[e]
        ins.then_inc(s, 1)
        self.ninstr += 1
        self._record(s, v, reads, writes, waw)
        return ins

    def dma(self, e, out, in_, sbuf, reads=(), writes=(), waw=True):
        self._wait(e, self._collect(reads, writes, waw))
        if sbuf.dsem is None or sbuf.dcnt >= SEM_LIMIT:
            if sbuf.dsem is not None and sbuf in self.dbufs:
                self.dbufs.remove(sbuf)
            if self.free_dsems:
                sbuf.dsem, sbuf.dcnt = self.free_dsems.pop()
            else:
                sbuf.dsem = self._alloc("d_" + sbuf.name)
                sbuf.dcnt = 0
            self.dbufs.append(sbuf)
        ins = self.eng[e].dma_start(out=out, in_=in_)
        sbuf.dcnt += 16
        s, v = sbuf.dsem, sbuf.dcnt
        ins.then_inc(s, 16)
        self.ninstr += 1
        self._record(s, v, reads, writes, waw)
        return ins

    def barrier(self):
        deps = {}
        for e in self.eng:
            if self.cnt[e] > 0:
                deps[self.sem[e]] = self.cnt[e]
        for b in self.dbufs:
            if b.dsem is not None and b.dcnt > 0:
                deps[b.dsem] = b.dcnt
        for e in self.eng:
            seen = self.seen[e]
            for s, v in deps.items():
                if seen.get(s, 0) >= v:
                    continue
                self.eng[e].wait_ge(s, v)
                self.ninstr += 1
                seen[s] = v
        for b in self.dbufs:
            if b.dsem is not None and b.dcnt < SEM_LIMIT - 4000:
                self.free_dsems.append((b.dsem, b.dcnt))
            b.dsem = None
            b.dcnt = 0
        self.dbufs = []


def ssl(s0, n, d):
    return slice(s0, s0 + (n - 1) * d + 1, d)


class Ring:
    def __init__(self, items):
        self.items = items
        self.i = 0

    def next(self):
        it = self.items[self.i % len(self.items)]
        self.i += 1
        return it


def build_program(n_layers=DEPTH, debug=False, stop_after=None):
    nc = bass.Bass("TRN2", target_bir_lowering=False)
    S = Sched(nc)

    _uid = [0]
    _orig_sbuf_tensor = nc.sbuf_tensor

    def _sbuf_tensor(name, shape, dt):
        _uid[0] += 1
        return _orig_sbuf_tensor(f"{name}_u{_uid[0]}", shape, dt)

    def din(name, shape, dt=F32):
        return nc.dram_tensor(name, list(shape), dt, kind="ExternalInput").ap()

    def dscr(name, shape, dt):
        kind = "ExternalOutput" if debug else "Internal"
        return nc.dram_tensor(name, list(shape), dt, kind=kind).ap()

    x_in = din("x", [S_LEN, D])
    w_in = din("w_in", [DEPTH, D, IN_COLS])
    p_a = din("p_a", [DEPTH, 256, D])
    p_b = din("p_b", [DEPTH, 512, D])
    p_c = din("p_c", [DEPTH, 512, D])
    w_out = din("w_out", [DEPTH, D, D])
    w_gu = din("w_gu", [DEPTH, D, 2 * FFN])
    w_down = din("w_down", [DEPTH, FFN, D])
    biasA_d = din("biasA", [128, 12, 256])
    strip_d = din("stripC", [8, 128, STRIP_W])
    oh16_d = din("oh16", [16, S_LEN], BF16)
    gmask_d = din("gmask", [128, 2, 512])
    b31_d = din("b31c", [128, 8])
    vec_d = din("vecs", [128, DEPTH, 16])
    lruv_d = din("lruv", [128, DEPTH, 4, 8])
    lrubd_d = din("lrubd", [DEPTH, 2, 128, 4, 128])
    gfin_d = din("g_final_b", [128, D])
    out_d = nc.dram_tensor("out", [S_LEN, D], F32, kind="ExternalOutput").ap()

    w_in_bf = dscr("w_in_bf", [DEPTH, D, IN_COLS], BF16)
    pcat_bf = dscr("pcat_bf", [DEPTH, 1280, D], BF16)
    w_out_bf = dscr("w_out_bf", [DEPTH, D, D], BF16)
    w_gu_bf = dscr("w_gu_bf", [DEPTH, D, 2 * FFN], BF16)
    w_down_bf = dscr("w_down_bf", [DEPTH, FFN, D], BF16)
    xs = [dscr("xs0", [S_LEN, D], F32), dscr("xs1", [S_LEN, D], F32)]
    oT_d = dscr("oT", [1280, S_LEN], BF16)
    gT_d = dscr("gT", [3 * D, S_LEN], BF16)

    B_wbf = [Buf(f"wbf{l}") for l in range(DEPTH)]
    B_xs = [Buf("xs0"), Buf("xs1")]
    B_oT = Buf("oT")
    B_gT = Buf("gT")
    B_xin = Buf("xin")
    B_out = Buf("out")
    B_const = Buf("const_in")

    ident = nc.alloc_sbuf_tensor("ident", [128, 128], BF16)
    B_ident = Buf("ident")
    ones = nc.alloc_sbuf_tensor("ones", [128, 64], F32)
    B_ones = Buf("ones")
    vecs = nc.alloc_sbuf_tensor("vecs_sb", [128, DEPTH, 16], F32)
    B_vecs = Buf("vecs")
    epsT = nc.alloc_sbuf_tensor("epsT", [128, 1], F32)
    NW = 6
    wring = Ring([(nc.alloc_sbuf_tensor(f"wslot{i}", [128, NDC, 128], BF16), Buf(f"wslot{i}")) for i in range(NW)])
    pbanks = [(nc.alloc_psum_tensor(f"pb{i}", [128, 512], F32), Buf(f"pb{i}")) for i in range(8)]
    ptr = pbanks[7][0][:, :].bitcast(BF16)
    B_ptr = pbanks[7][1]
    pring = Ring(pbanks[0:2])
    sring = Ring(pbanks[2:6])
    aring = Ring(pbanks[6:8])
    allring = Ring(pbanks[0:6])

    S.op("pool", lambda: nc.gpsimd.memset(ident[:], 1.0), writes=[B_ident])
    S.op("pool", lambda: nc.gpsimd.affine_select(out=ident[:], in_=ident[:], pattern=[[-1, 128]],
                                                  compare_op=ALU.is_equal, fill=0.0, base=0, channel_multiplier=1),
         reads=[B_ident], writes=[B_ident])
    S.op("pool", lambda: nc.gpsimd.memset(ones[:], 1.0), writes=[B_ones])
    S.op("pool", lambda: nc.gpsimd.memset(epsT[:], EPS), writes=[B_vecs])
    S.dma("sp", vecs[:], vec_d, B_vecs, writes=[B_vecs], waw=False)

    ev_i = [0]

    def evac_eng():
        ev_i[0] += 1
        return "act" if ev_i[0] % 2 else "dve"

    def copy(e, out, in_, reads, writes, waw=True):
        if e == "act":
            return S.op("act", lambda: nc.scalar.copy(out=out, in_=in_), reads, writes, waw)
        if e == "dve":
            return S.op("dve", lambda: nc.vector.tensor_copy(out=out, in_=in_), reads, writes, waw)
        return S.op("pool", lambda: nc.gpsimd.tensor_copy(out=out, in_=in_), reads, writes, waw)

    def mm(out, lhsT, rhs, start, stop, reads, writes):
        return S.op("pe", lambda: nc.tensor.matmul(out, lhsT=lhsT, rhs=rhs, start=start, stop=stop), reads, writes)

    def load_w(src2d, col0, ncols, B_src):
        wt, bw = wring.next()
        S.dma("sp", wt[:, :, 0:ncols], src2d[:, col0:col0 + ncols].rearrange("(dc p) m -> p dc m", p=128),
              bw, reads=(list(B_src) if isinstance(B_src, (list, tuple)) else [B_src]), writes=[bw])
        return wt, bw

    class WQ:
        def __init__(self, specs, depth):
            self.specs = list(specs)
            self.depth = depth
            self.i = 0
            self.q = []

        def get(self):
            while len(self.q) < self.depth + 1 and self.i < len(self.specs):
                src2d, col0, n, B_src = self.specs[self.i]
                self.i += 1
                self.q.append(load_w(src2d, col0, n, B_src))
            return self.q.pop(0)

    CW = 1024

    def tiles2d(src, dst, rows, cols):
        out = []
        for r0 in range(0, rows, 128):
            for c0 in range(0, cols, CW):
                n = min(CW, cols - c0)
                out.append((src[r0:r0 + 128, c0:c0 + n], dst[r0:r0 + 128, c0:c0 + n], n))
        return out

    def w_in_tiles(l, hi):
        out = []
        for r0 in range(0, D, 128):
            for c0 in range(0, IN_COLS, CW):
                if (c0 >= 4096) != hi:
                    continue
                n = min(CW, IN_COLS - c0)
                out.append((w_in[l][r0:r0 + 128, c0:c0 + n], w_in_bf[l][r0:r0 + 128, c0:c0 + n], n))
        return out

    B_whi0 = Buf("whi0")

    def weight_tiles(l, with_w_in=True, with_rest=True):
        t = []
        if with_w_in:
            t += tiles2d(w_in[l], w_in_bf[l], D, IN_COLS)
        if with_rest:
            t += tiles2d(p_a[l], pcat_bf[l, 0:256], 256, D)
            t += tiles2d(p_b[l], pcat_bf[l, 256:768], 512, D)
            t += tiles2d(p_c[l], pcat_bf[l, 768:1280], 512, D)
            t += tiles2d(w_out[l], w_out_bf[l], D, D)
            t += tiles2d(w_gu[l], w_gu_bf[l], D, 2 * FFN)
            t += tiles2d(w_down[l], w_down_bf[l], FFN, D)
        return t

    def make_pump(st, tiles, B_dst, cast_engs, store_q, nbuf=3):
        stg = Ring([(st.enter_context(_sbuf_tensor(f"cst{i}", [128, CW], F32)), Buf(f"cst{i}")) for i in range(nbuf)])
        stb = Ring([(st.enter_context(_sbuf_tensor(f"csb{i}", [128, CW], BF16)), Buf(f"csb{i}")) for i in range(nbuf)])
        ce = Ring(list(cast_engs))
        todo = list(tiles)
        loaded = []

        def load_one():
            src, dst, n = todo.pop(0)
            a_, ba = stg.next()
            S.dma("sp", a_[:, 0:n], src, ba, writes=[ba])
            loaded.append((a_, ba, dst, n))

        def pump(k):
            for _ in range(k):
                if not todo and not loaded:
                    return
                while todo and len(loaded) < nbuf - 1:
                    load_one()
                a_, ba, dst, n = loaded.pop(0)
                b_, bb = stb.next()
                copy(ce.next(), b_[:, 0:n], a_[:, 0:n], [ba], [bb])
                S.dma(store_q, dst, b_[:, 0:n], bb, reads=[bb], writes=[B_dst], waw=False)
                if todo:
                    load_one()

        def drain():
            while todo or loaded:
                pump(1)
        return pump, drain

    def phase_cast():
        with contextlib.ExitStack() as st:
            _, drain = make_pump(st, weight_tiles(0, True, False), B_wbf[0], ["pool", "dve", "act"], "act", nbuf=4)
            drain()
            S.barrier()

    tbanks = [(pbanks[7][0][:, :].bitcast(BF16), pbanks[7][1]), (pbanks[6][0][:, :].bitcast(BF16), pbanks[6][1])]
    tb_i = [0]

    def norm_a(xt, bxt, xn, bxn, small, bsmall):
        ssq = small[:, 0:1]
        rstd = small[:, 1:2]
        S.op("act", lambda: nc.scalar.activation(out=xn[:, :], in_=xt, func=AF.Square, accum_out=ssq),
             reads=[bxt], writes=[bxn, bsmall])
        S.op("act", lambda: nc.scalar.activation(out=rstd, in_=ssq, func=AF.Sqrt, bias=epsT[:, 0:1], scale=1.0 / D),
             reads=[bsmall, B_vecs], writes=[bsmall])
        S.op("dve", lambda: nc.vector.reciprocal(out=rstd, in_=rstd), reads=[bsmall], writes=[bsmall])

    def norm_b(xt, bxt, xn, bxn, small, bsmall):
        S.op("act", lambda: nc.scalar.activation(out=xn[:, :], in_=xt, func=AF.Copy, scale=small[:, 1:2]),
             reads=[bxt, bsmall], writes=[bxn])

    def transpose_part(xn, bxn, dstT, col0, B_dst, gcol, l):
        tb_i[0] += 1
        tp, B_tp = tbanks[tb_i[0] % 2]
        for dc in range(NDC):
            S.op("pe", lambda dc=dc: nc.tensor.transpose(out=tp[:, dc * 128:(dc + 1) * 128],
                                                        in_=xn[:, dc * 128:(dc + 1) * 128], identity=ident[:]),
                 reads=[bxn, B_ident], writes=[B_tp])
        g_b = vecs[:, l, gcol:gcol + 8].unsqueeze(2).to_broadcast([128, NDC, 128])
        S.op("dve", lambda: nc.vector.tensor_tensor(out=dstT[:, :, col0:col0 + 128],
                                                     in0=tp.rearrange("p (c t) -> p c t", c=NDC),
                                                     in1=g_b, op=ALU.mult),
             reads=[B_tp, B_vecs], writes=[B_dst], waw=False)

    def layer(l, x_src, B_xsrc, x_dst, B_xdst, last):
        Bw = B_wbf[l]
        rest0 = weight_tiles(0, False, True) if l == 0 else []
        next_tiles = weight_tiles(l + 1, True, True) if l + 1 < n_layers else []
        with contextlib.ExitStack() as att:
            hT = att.enter_context(_sbuf_tensor("hT", [128, NDC, S_LEN], BF16))
            B_hT = Buf("hT")

            with contextlib.ExitStack() as st:
                xr = Ring([(st.enter_context(_sbuf_tensor(f"p1x{i}", [128, D], F32)), Buf(f"p1x{i}")) for i in range(4)])
                nr = Ring([(st.enter_context(_sbuf_tensor(f"p1n{i}", [128, D], BF16)), Buf(f"p1n{i}")) for i in range(4)])
                sr = Ring([(st.enter_context(_sbuf_tensor(f"p1s{i}", [128, 2], F32)), Buf(f"p1s{i}")) for i in range(4)])
                pend = []
                pend2 = []
                pump_i, drain_i = (None, None)
                if l == 0:
                    pump_i, drain_i = make_pump(st, w_in_tiles(0, True), B_whi0, ["dve", "act"], "pool")
                items = {}
                for t in range(NT + 2):
                    if pump_i is not None:
                        pump_i(1)
                    if t < NT:
                        xt, bxt = xr.next()
                        xn, bxn = nr.next()
                        sm, bsm = sr.next()
                        S.dma("sp", xt[:], x_src[t * 128:(t + 1) * 128, :], bxt, reads=[B_xsrc], writes=[bxt])
                        norm_a(xt[:, :], bxt, xn, bxn, sm, bsm)
                        items[t] = (xt, bxt, xn, bxn, sm, bsm)
                    if 0 <= t - 1 < NT:
                        xt_, bxt_, xn_, bxn_, sm_, bsm_ = items[t - 1]
                        norm_b(xt_[:, :], bxt_, xn_, bxn_, sm_, bsm_)
                    if 0 <= t - 2 < NT:
                        xt_, bxt_, xn_, bxn_, sm_, bsm_ = items.pop(t - 2)
                        transpose_part(xn_, bxn_, hT, (t - 2) * 128, B_hT, 0, l)
                if drain_i is not None:
                    drain_i()
                S.barrier()
            if stop_after == "p1":
                return hT

            def proj_fm(col0, M, dst_fn, dst_bufs, func=None, e_fixed=None, ntc=NTC, tc0=0, ring=None, w=None):
                wt, bw = w if w is not None else load_w(w_in_bf[l], col0, M, Bw)
                for tc in range(tc0, tc0 + ntc):
                    pb, bpb = (ring or pring).next()
                    for dc in range(NDC):
                        mm(pb[0:M, :], wt[:, dc, 0:M], hT[:, dc, tc * 512:(tc + 1) * 512], dc == 0, dc == NDC - 1,
                           [bw, B_hT], [bpb])
                    dst = dst_fn(tc)
                    if func is not None:
                        S.op("act", lambda: nc.scalar.activation(out=dst, in_=pb[0:M, :], func=func),
                             reads=[bpb], writes=dst_bufs, waw=False)
                    else:
                        copy(e_fixed or evac_eng(), dst, pb[0:M, :], [bpb], dst_bufs, waw=False)

            with contextlib.ExitStack() as st:
                gr = Ring([(st.enter_context(_sbuf_tensor(f"pgs{i}", [128, S_LEN], BF16)), Buf(f"pgs{i}")) for i in range(2)])
                pump_g, drain_g = (None, None)
                if l == 0:
                    pump_g, drain_g = make_pump(st, w_in_tiles(0, False) + rest0[:24], B_wbf[0], ["dve", "pool"], "pool")
                wq_g = WQ([(w_in_bf[l], GT + gi * 128, 128, (B_whi0 if l == 0 else Bw)) for gi in range(24)], 2)
                for gi in range(24):
                    if pump_g is not None:
                        pump_g(3)
                    gs, bgs = gr.next()
                    proj_fm(GT + gi * 128, 128, lambda tc: gs[:, tc * 512:(tc + 1) * 512], [bgs], func=AF.Sigmoid, ring=allring,
                            w=wq_g.get())
                    S.dma("sp", gT_d[gi * 128:(gi + 1) * 128, :], gs[:, :], bgs, reads=[bgs], writes=[B_gT], waw=False)
                if drain_g is not None:
                    drain_g()
                S.barrier()

            with contextlib.ExitStack() as st:
                biasA = st.enter_context(_sbuf_tensor("biasA_sb", [128, 12, 256], F32))
                B_bA = Buf("biasA")
                S.dma("sp", biasA[:], biasA_d, B_bA, writes=[B_bA])
                qTa = st.enter_context(_sbuf_tensor("qTa", [128, S_LEN], BF16))
                kTa = st.enter_context(_sbuf_tensor("kTa", [128, S_LEN], BF16))
                B_qTa, B_kTa = Buf("qTa"), Buf("kTa")
                Va = st.enter_context(_sbuf_tensor("Va", [128, NT, 2, 65], BF16))
                B_Va = Buf("Va")
                acc = st.enter_context(_sbuf_tensor("accA", [65, 2, S_LEN], F32))
                B_acc = Buf("accA")
                str_ = Ring([(st.enter_context(_sbuf_tensor(f"sTa{i}", [128, 256], F32)), Buf(f"sTa{i}")) for i in range(5)])
                ptr_ = Ring([(st.enter_context(_sbuf_tensor(f"pTa{i}", [128, 256], BF16)), Buf(f"pTa{i}")) for i in range(7)])
                rden = st.enter_context(_sbuf_tensor("rdenA", [65, S_LEN], F32))
                B_rden = Buf("rdenA")
                oas = st.enter_context(_sbuf_tensor("oas", [64, S_LEN], BF16))
                B_oas = Buf("oas")
                S.op("pool", lambda: nc.gpsimd.memset(Va[:, :, :, 64:65], 1.0), writes=[B_Va])
                pump_a, drain_a = (None, None)
                if l == 0:
                    pump_a, drain_a = make_pump(st, rest0[24:], B_wbf[0], ["pool"], "pool")
                wq_a = WQ([(w_in_bf[l], base + g_ * 256 + p_ * 128, 128, Bw) for p_ in range(2) for g_ in range(3)
                           for base in (QA, KA, VA)], 2)
                for p in range(2):
                    for g, d in enumerate((1, 4, 16)):
                        fo = g * 256 + p * 128
                        nb = NT // d
                        proj_fm(QA + fo, 128, lambda tc: qTa[:, tc * 512:(tc + 1) * 512], [B_qTa], w=wq_a.get())
                        proj_fm(KA + fo, 128, lambda tc: kTa[:, tc * 512:(tc + 1) * 512], [B_kTa], w=wq_a.get())
                        wv, bwv = wq_a.get()
                        for kt0 in range(0, NT, 4):
                            pb, bpb = pring.next()
                            for i in range(4):
                                kt = kt0 + i
                                r, m = kt // nb, kt % nb
                                s0 = m * 128 * d + r
                                for dc in range(NDC):
                                    mm(pb[:, i * 128:(i + 1) * 128], hT[:, dc, ssl(s0, 128, d)], wv[:, dc, :],
                                       dc == 0, dc == NDC - 1, [bwv, B_hT], [bpb])
                            copy(evac_eng(), Va[:, kt0:kt0 + 4, :, 0:64],
                                 pb[:, :].rearrange("p (k h e) -> p k h e", k=4, h=2), [bpb], [B_Va], waw=False)
                        items = [(hh, r, m) for hh in range(2) for r in range(d) for m in range(nb)]
                        nbb = min(4, nb)
                        stt = {}

                        def a_stage1(hh, r, m):
                            hd = g * 4 + p * 2 + hh
                            hs = slice(hh * 64, (hh + 1) * 64)
                            nq = 256 if m < nb - 1 else 128
                            s0 = m * 128 * d + r
                            ps, bps = sring.next()
                            mm(ps[:, 0:nq], kTa[hs, ssl(s0, 128, d)], qTa[hs, ssl(s0, nq, d)], True, True,
                               [B_kTa, B_qTa], [bps])
                            sT, bsT = str_.next()
                            S.op("dve", lambda: nc.vector.scalar_tensor_tensor(
                                out=sT[:, 0:nq], in0=ps[:, 0:nq], scalar=SCALE, in1=biasA[:, hd, 0:nq],
                                op0=ALU.mult, op1=ALU.add), reads=[bps, B_bA], writes=[bsT])
                            pT, bpT = ptr_.next()
                            S.op("act", lambda: nc.scalar.activation(out=pT[:, 0:nq], in_=sT[:, 0:nq], func=AF.Exp),
                                 reads=[bsT], writes=[bpT])
                            return (hh, r, m, pT, bpT)

                        def a_stage2(hh, r, m, pT, bpT):
                            if m % nbb == 0:
                                stt["pacc"] = aring.next()
                            pa, bpa = stt["pacc"]
                            cs = slice((m % nbb) * 128, (m % nbb + 1) * 128)
                            kt = r * nb + m
                            if m > 0:
                                ppT, bppT = stt["prev"]
                                mm(pa[0:65, cs], Va[:, kt - 1, hh, :], ppT[:, 128:256], True, False, [B_Va, bppT], [bpa])
                                mm(pa[0:65, cs], Va[:, kt, hh, :], pT[:, 0:128], False, True, [B_Va, bpT], [bpa])
                            else:
                                mm(pa[0:65, cs], Va[:, kt, hh, :], pT[:, 0:128], True, True, [B_Va, bpT], [bpa])
                            stt["prev"] = (pT, bpT)
                            if m % nbb == nbb - 1:
                                m0 = m - (nbb - 1)
                                t0 = m0 * 128 * d + r
                                acc_ap = acc[:, hh, ssl(t0, nbb * 128, d)]
                                if g == 0:
                                    copy("act", acc_ap, pa[0:65, 0:nbb * 128], [bpa], [B_acc], waw=False)
                                else:
                                    S.op("dve", lambda: nc.vector.tensor_tensor(
                                        out=acc_ap, in0=acc_ap, in1=pa[0:65, 0:nbb * 128], op=ALU.add),
                                        reads=[bpa, B_acc], writes=[B_acc], waw=False)

                        LA_A = 3
                        infl = []
                        for i in range(len(items) + LA_A):
                            if i < len(items):
                                infl.append(a_stage1(*items[i]))
                            if i >= LA_A:
                                a_stage2(*infl.pop(0))
                            if pump_a is not None and i % 6 == 3:
                                pump_a(1)
                    for hh in range(2):
                        S.op("act", lambda: nc.scalar.activation(out=rden[64:65, :], in_=acc[64:65, hh, :], func=AF.Ln),
                             reads=[B_acc], writes=[B_rden])
                        S.op("act", lambda: nc.scalar.activation(out=rden[64:65, :], in_=rden[64:65, :], func=AF.Exp, scale=-1.0),
                             reads=[B_rden], writes=[B_rden])
                        for tc in range(NTC):
                            cs = slice(tc * 512, (tc + 1) * 512)
                            pb, bpb = pring.next()
                            mm(pb[0:64, :], ones[64:65, 0:64], rden[64:65, cs], True, True, [B_ones, B_rden], [bpb])
                            S.op("dve", lambda: nc.vector.tensor_tensor(out=oas[:, cs], in0=acc[0:64, hh, cs], in1=pb[0:64, :],
                                                                         op=ALU.mult),
                                 reads=[bpb, B_acc], writes=[B_oas], waw=False)
                        slot = p * 2 + hh
                        S.dma("pool", oT_d[slot * 64:(slot + 1) * 64, :], oas[:, :], B_oas, reads=[B_oas], writes=[B_oT], waw=False)
                if drain_a is not None:
                    drain_a()
                S.barrier()

            with contextlib.ExitStack() as st:
                TB = 512
                lruv = st.enter_context(_sbuf_tensor("lruv", [128, 4, 8], F32))
                B_lv = Buf("lruv")
                S.dma("sp", lruv[:], lruv_d[:, l], B_lv, writes=[B_lv])
                bdf = st.enter_context(_sbuf_tensor("bdf", [128, 2, 4, 128], F32))
                bdb = st.enter_context(_sbuf_tensor("bdb", [128, 2, 4, 128], BF16))
                B_bd = Buf("bd")
                for j in range(2):
                    S.dma("sp", bdf[:, j], lrubd_d[l, j], B_bd, writes=[B_bd], waw=False)
                copy("dve", bdb[:], bdf[:], [B_bd], [B_bd])
                cl = st.enter_context(_sbuf_tensor("cl", [128, 4], F32))
                S.op("act", lambda: nc.scalar.activation(out=cl[:], in_=lruv[:, :, 7], func=AF.Exp, scale=-1.0),
                     reads=[B_lv], writes=[B_lv])
                S.op("act", lambda: nc.scalar.activation(out=cl[:], in_=cl[:], func=AF.Ln, bias=1.0, scale=1.0),
                     reads=[B_lv], writes=[B_lv])
                S.op("dve", lambda: nc.vector.tensor_scalar(out=cl[:], in0=cl[:], scalar1=-8.0, scalar2=None, op0=ALU.mult),
                     reads=[B_lv], writes=[B_lv])
                cl2 = st.enter_context(_sbuf_tensor("cl2", [128, 4], F32))
                S.op("dve", lambda: nc.vector.tensor_scalar(out=cl2[:], in0=cl[:], scalar1=2.0, scalar2=None, op0=ALU.mult),
                     reads=[B_lv], writes=[B_lv])
                wxg = st.enter_context(_sbuf_tensor("wxg", [128, 8, NDC, 128], BF16))
                B_wxg = [Buf(f"wxg{i}") for i in range(8)]
                for c in range(4):
                    for j, col in enumerate((XB, GB)):
                        S.dma("sp", wxg[:, j * 4 + c], w_in_bf[l][:, col + c * 128:col + (c + 1) * 128].rearrange("(dc p) m -> p dc m", p=128),
                              B_wxg[j * 4 + c], reads=[Bw], writes=[B_wxg[j * 4 + c]])

                def tl(name, dt=F32, w=TB):
                    return [(st.enter_context(_sbuf_tensor(f"{name}{c}", [128, w], dt)), Buf(f"{name}{c}")) for c in range(4)]
                XB2 = [tl("xbh0_", F32, TB + 3), tl("xbh1_", F32, TB + 3)]
                GB2 = [tl("gb0_"), tl("gb1_")]
                XC, XCB, RR, IG, AA, HH = tl("xc"), tl("xcb", BF16), tl("r"), tl("ig"), tl("a"), tl("hh")
                UU, OB = tl("u"), tl("ob", BF16)
                A2 = tl("a2")
                BT = A2
                print(f"[kernel] PB phase l={l}: sbuf bytes remaining {nc.sbuf_bytes_remaining}")
                HALO, HCAR = tl("halo", F32, 3), tl("hcar", F32, 1)
                CH = range(4)
                def s1(tb):
                    tcs = slice(tb * TB, (tb + 1) * TB)
                    for c in CH:
                        xb, bxb = XB2[tb % 2][c]
                        gb, bgb = GB2[tb % 2][c]
                        pb, bpb = allring.next()
                        for dc in range(NDC):
                            mm(pb[:, :], wxg[:, c, dc, :], hT[:, dc, tcs], dc == 0, dc == NDC - 1, [B_wxg[c], B_hT], [bpb])
                        copy("act", xb[:, 3:3 + TB], pb[:, :], [bpb], [bxb], waw=False)
                        pb, bpb = allring.next()
                        for dc in range(NDC):
                            mm(pb[:, :], wxg[:, 4 + c, dc, :], hT[:, dc, tcs], dc == 0, dc == NDC - 1, [B_wxg[4 + c], B_hT], [bpb])
                        copy("dve", gb[:, :], pb[:, :], [bpb], [bgb])

                s1(0)
                for tb in range(S_LEN // TB):
                    tcs = slice(tb * TB, (tb + 1) * TB)
                    XBt = XB2[tb % 2]
                    GBt = GB2[tb % 2]
                    if tb + 1 < S_LEN // TB:
                        s1(tb + 1)
                    for c in CH:
                        xb, bxb = XBt[c]
                        if tb == 0:
                            S.op("pool", lambda: nc.gpsimd.memset(xb[:, 0:3], 0.0), writes=[bxb], waw=False)
                        else:
                            copy("pool", xb[:, 0:3], HALO[c][0][:, :], [HALO[c][1]], [bxb], waw=False)
                    for c in CH:
                        xb, bxb = XBt[c]
                        xc, bxc = XC[c]
                        S.op("dve", lambda: nc.vector.tensor_scalar(out=xc[:], in0=xb[:, 0:TB], scalar1=lruv[:, c, 0:1],
                                                                     scalar2=lruv[:, c, 4:5], op0=ALU.mult, op1=ALU.add),
                             reads=[bxb, B_lv], writes=[bxc])
                    for i in range(1, 4):
                        for c in CH:
                            xb, bxb = XBt[c]
                            xc, bxc = XC[c]
                            S.op("dve", lambda: nc.vector.scalar_tensor_tensor(out=xc[:], in0=xb[:, i:i + TB],
                                                                                scalar=lruv[:, c, i:i + 1], in1=xc[:],
                                                                                op0=ALU.mult, op1=ALU.add),
                                 reads=[bxb, B_lv, bxc], writes=[bxc])
                    for c in CH:
                        copy("pool", HALO[c][0][:, :], XBt[c][0][:, TB:TB + 3], [XBt[c][1]], [HALO[c][1]])
                        copy("act", XCB[c][0][:], XC[c][0][:], [XC[c][1]], [XCB[c][1]])
                    for c in CH:
                        xcb, bxcb = XCB[c]
                        r_, br_ = RR[c]
                        ig, big = IG[c]
                        pb, bpb = allring.next()
                        mm(pb[:, :], bdb[:, 0, c, :], xcb[:, :], True, True, [B_bd, bxcb], [bpb])
                        S.op("act", lambda: nc.scalar.activation(out=r_[:, :], in_=pb[:, :], func=AF.Sigmoid, bias=lruv[:, c, 5:6]),
                             reads=[bpb, B_lv], writes=[br_])
                        pb, bpb = allring.next()
                        mm(pb[:, :], bdb[:, 1, c, :], xcb[:, :], True, True, [B_bd, bxcb], [bpb])
                        S.op("act", lambda: nc.scalar.activation(out=ig[:, :], in_=pb[:, :], func=AF.Sigmoid, bias=lruv[:, c, 6:7]),
                             reads=[bpb, B_lv], writes=[big])
                    for c in CH:
                        S.op("act", lambda: nc.scalar.activation(out=AA[c][0][:], in_=RR[c][0][:], func=AF.Exp, scale=cl[:, c:c + 1]),
                             reads=[RR[c][1], B_lv], writes=[AA[c][1]])
                    for c in CH:
                        S.op("act", lambda: nc.scalar.activation(out=A2[c][0][:], in_=RR[c][0][:], func=AF.Exp, scale=cl2[:, c:c + 1]),
                             reads=[RR[c][1], B_lv], writes=[A2[c][1]])
                    for c in CH:
                        S.op("act", lambda: nc.scalar.activation(out=RR[c][0][:], in_=A2[c][0][:], func=AF.Sqrt, bias=1.0, scale=-1.0),
                             reads=[A2[c][1], RR[c][1]], writes=[RR[c][1]])
                    for c in CH:
                        S.op("pool", lambda: nc.gpsimd.tensor_tensor(out=IG[c][0][:], in0=IG[c][0][:], in1=XC[c][0][:], op=ALU.mult),
                             reads=[IG[c][1], XC[c][1]], writes=[IG[c][1]])
                    for c in CH:
                        S.op("dve", lambda: nc.vector.tensor_tensor(out=BT[c][0][:], in0=IG[c][0][:], in1=RR[c][0][:], op=ALU.mult),
                             reads=[IG[c][1], RR[c][1]], writes=[BT[c][1]])
                    for c in CH:
                        init = 0.0 if tb == 0 else HCAR[c][0][:, 0:1]
                        rd = [AA[c][1], BT[c][1]] + ([] if tb == 0 else [HCAR[c][1]])
                        S.op("dve", lambda: nc.vector.tensor_tensor_scan(out=HH[c][0][:], data0=AA[c][0][:], data1=BT[c][0][:],
                                                                          initial=init, op0=ALU.mult, op1=ALU.add),
                             reads=rd, writes=[HH[c][1]])
                    for c in CH:
                        copy("act", HCAR[c][0][:, 0:1], HH[c][0][:, TB - 1:TB], [HH[c][1]], [HCAR[c][1]])
                    for c in CH:
                        S.op("act", lambda: nc.scalar.activation(out=UU[c][0][:], in_=GBt[c][0][:], func=AF.Square, scale=0.044715 ** 0.5),
                             reads=[GBt[c][1]], writes=[UU[c][1]])
                    for c in CH:
                        S.op("dve", lambda: nc.vector.scalar_tensor_tensor(out=UU[c][0][:], in0=UU[c][0][:], scalar=1.0, in1=GBt[c][0][:],
                                                                            op0=ALU.add, op1=ALU.mult),
                             reads=[UU[c][1], GBt[c][1]], writes=[UU[c][1]])
                    for c in CH:
                        S.op("act", lambda: nc.scalar.activation(out=UU[c][0][:], in_=UU[c][0][:], func=AF.Sigmoid, scale=1.5957691216057308),
                             reads=[UU[c][1]], writes=[UU[c][1]])
                    for c in CH:
                        S.op("dve", lambda: nc.vector.tensor_tensor(out=UU[c][0][:], in0=UU[c][0][:], in1=GBt[c][0][:], op=ALU.mult),
                             reads=[UU[c][1], GBt[c][1]], writes=[UU[c][1]])
                    for c in CH:
                        S.op("dve", lambda: nc.vector.tensor_tensor(out=OB[c][0][:], in0=UU[c][0][:], in1=HH[c][0][:], op=ALU.mult),
                             reads=[UU[c][1], HH[c][1]], writes=[OB[c][1]])
                    for c in CH:
                        S.dma("sp", oT_d[256 + c * 128:256 + (c + 1) * 128, tcs], OB[c][0][:, :], OB[c][1],
                              reads=[OB[c][1]], writes=[B_oT], waw=False)
                S.barrier()

            with contextlib.ExitStack() as st:
                qr = Ring([(st.enter_context(_sbuf_tensor(f"qTc{i}", [80, S_LEN], BF16)), Buf(f"qTc{i}")) for i in range(2)])
                kr = Ring([(st.enter_context(_sbuf_tensor(f"kTc{i}", [80, S_LEN], BF16)), Buf(f"kTc{i}")) for i in range(2)])
                for kt_, bk_ in kr.items:
                    S.dma("sp", kt_[64:80, :], oh16_d, bk_, writes=[bk_], waw=False)
                vr = Ring([(st.enter_context(_sbuf_tensor(f"Vc{i}", [128, NT, 65], BF16)), Buf(f"Vc{i}")) for i in range(2)])
                for v_, bv_ in vr.items:
                    S.op("pool", lambda v_=v_: nc.gpsimd.memset(v_[:, :, 64:65], 1.0), writes=[bv_], waw=False)
                stripr = Ring([(st.enter_context(_sbuf_tensor(f"strip{i}", [128, STRIP_W], F32)), Buf(f"strip{i}")) for i in range(2)])
                b31 = st.enter_context(_sbuf_tensor("b31", [128, 8], F32))
                B_b31 = Buf("b31")
                S.dma("sp", b31[:], b31_d, B_b31, writes=[B_b31])
                gmask = st.enter_context(_sbuf_tensor("gmask", [128, 2, 512], F32))
                B_gmk = Buf("gmask")
                S.dma("sp", gmask[:], gmask_d, B_gmk, writes=[B_gmk])
                kms = st.enter_context(_sbuf_tensor("kms", [64, 16], F32))
                kmb = st.enter_context(_sbuf_tensor("kmb", [64, 16], BF16))
                B_km = Buf("km")
                gm = st.enter_context(_sbuf_tensor("gm", [128, 512], F32))
                B_gm = Buf("gm")
                m8a = st.enter_context(_sbuf_tensor("m8a", [128, NT, 8], F32))
                B_m8 = Buf("m8a")
                ltt = st.enter_context(_sbuf_tensor("ltt", [128, 512], F32))
                B_lt = Buf("ltt")
                negpad = st.enter_context(_sbuf_tensor("negpad", [128, NT, 80], BF16))
                B_np = Buf("negpad")
                S.op("pool", lambda: nc.gpsimd.memset(negpad[:], 0.0), writes=[B_np])
                tmr = Ring([(st.enter_context(_sbuf_tensor(f"tmpc{i}", [128, 512], F32)), Buf(f"tmpc{i}")) for i in range(5)])
                pTr = Ring([(st.enter_context(_sbuf_tensor(f"pTc{i}", [128, 512], BF16)), Buf(f"pTc{i}")) for i in range(6)])
                numr = Ring([(st.enter_context(_sbuf_tensor(f"numc{i}", [65, 512], F32)), Buf(f"numc{i}")) for i in range(3)])
                rdr = Ring([(st.enter_context(_sbuf_tensor(f"rdc{i}", [65, 512], F32)), Buf(f"rdc{i}")) for i in range(3)])
                ocr = Ring([(st.enter_context(_sbuf_tensor(f"ocs{i}", [64, S_LEN], BF16)), Buf(f"ocs{i}")) for i in range(2)])
                LA = 3
                print(f"[kernel] PC phase l={l}: sbuf bytes remaining {nc.sbuf_bytes_remaining}")
                pump_c, drain_c = (None, None)
                if l + 1 < n_layers:
                    pump_c, drain_c = make_pump(st, next_tiles, B_wbf[l + 1], ["pool"], "pool", nbuf=2)

                wq_c = WQ([(w_in_bf[l], base + h_ * 64, 64, ([Bw, B_whi0] if l == 0 else Bw)) for h_ in range(8) for base in (QC, KC, VC)], 2)

                def prologue(h):
                    qT, bq = qr.next()
                    kT, bk = kr.next()
                    Vc, bV = vr.next()
                    strip, bstrip = stripr.next()
                    ctx = dict(qT=qT, bq=bq, kT=kT, bk=bk, Vc=Vc, bV=bV, strip=strip, bstrip=bstrip)
                    S.dma("sp", strip[:], strip_d[h], bstrip, writes=[bstrip])
                    for (col, dstT, bdst) in ((QC + h * 64, qT, bq), (KC + h * 64, kT, bk)):
                        wt, bw = wq_c.get()
                        for tc in range(NTC):
                            pb, bpb = pring.next()
                            for dc in range(NDC):
                                mm(pb[0:64, :], wt[:, dc, 0:64], hT[:, dc, tc * 512:(tc + 1) * 512], dc == 0, dc == NDC - 1,
                                   [bw, B_hT], [bpb])
                            copy(evac_eng(), dstT[0:64, tc * 512:(tc + 1) * 512], pb[0:64, :], [bpb], [bdst], waw=False)
                            yield ctx
                    wv, bwv = wq_c.get()
                    for t0 in range(0, NT, 8):
                        pb, bpb = pring.next()
                        for i in range(8):
                            t = t0 + i
                            for dc in range(NDC):
                                mm(pb[:, i * 64:(i + 1) * 64], hT[:, dc, t * 128:(t + 1) * 128], wv[:, dc, 0:64],
                                   dc == 0, dc == NDC - 1, [bwv, B_hT], [bpb])
                        copy(evac_eng(), Vc[:, t0:t0 + 8, 0:64], pb[:, :].rearrange("p (k e) -> p k e", k=8), [bpb], [bV], waw=False)
                        yield ctx
                    for q4 in range(4):
                        S.op("dve", lambda: nc.vector.tensor_reduce(
                            out=kms[:, q4 * 4:(q4 + 1) * 4], in_=kT[0:64, q4 * 1024:(q4 + 1) * 1024].rearrange("p (n k) -> p n k", k=256),
                            axis=AX.X, op=ALU.add), reads=[bk], writes=[B_km], waw=(q4 == 0))
                        yield ctx
                    S.op("dve", lambda: nc.vector.tensor_scalar(out=kmb[:, :], in0=kms[:, :], scalar1=1.0 / 256.0, scalar2=None,
                                                                 op0=ALU.mult),
                         reads=[B_km], writes=[B_km])
                    yield ctx
                    pg, bpg = pring.next()
                    for qt in range(NT):
                        mm(pg[:, qt * 16:(qt + 1) * 16], qT[0:64, qt * 128:(qt + 1) * 128], kmb[:, :], True, True, [bq, B_km], [bpg])
                    yield ctx
                    S.op("dve", lambda: nc.vector.tensor_tensor(out=gm[:, :], in0=pg[:, :], in1=gmask[:, 0, :], op=ALU.add),
                         reads=[bpg, B_gmk], writes=[B_gm])
                    yield ctx
                    for qt in range(NT):
                        S.op("dve", lambda qt=qt: nc.vector.max(out=m8a[:, qt, :], in_=gm[:, qt * 16:(qt + 1) * 16]),
                             reads=[B_gm], writes=[B_m8], waw=(qt == 0))
                        if qt % 4 == 3:
                            yield ctx
                    S.op("dve", lambda: nc.vector.tensor_tensor(out=ltt[:, :].rearrange("p (q n) -> p q n", n=16),
                                                                 in0=gm[:, :].rearrange("p (q n) -> p q n", n=16),
                                                                 in1=m8a[:, :, 2:3].to_broadcast([128, NT, 16]), op=ALU.is_lt),
                         reads=[B_gm, B_m8], writes=[B_lt])
                    yield ctx
                    S.op("dve", lambda: nc.vector.tensor_tensor(out=negpad[:, :, 64:80],
                                                                 in0=ltt[:, :].rearrange("p (q n) -> p q n", n=16),
                                                                 in1=gmask[:, 1, :].rearrange("p (q n) -> p q n", n=16), op=ALU.mult),
                         reads=[B_lt, B_gmk], writes=[B_np])
                    yield ctx
                    for g4 in range(NT // 4):
                        pt, bpt = sring.next()
                        for i in range(4):
                            qt = g4 * 4 + i
                            mm(pt[0:80, i * 128:(i + 1) * 128], negpad[:, qt, :], ident[:, :], True, True, [B_np, B_ident], [bpt])
                        copy(evac_eng(), qT[64:80, g4 * 512:(g4 + 1) * 512], pt[64:80, :], [bpt], [bq], waw=False)
                        yield ctx

                def run_all(gen):
                    ctx = None
                    for ctx in gen:
                        pass
                    return ctx

                cur = run_all(prologue(0))
                for h in range(8):
                    qT, bq, kT, bk, Vc, bV = cur["qT"], cur["bq"], cur["kT"], cur["bk"], cur["Vc"], cur["bV"]
                    strip, bstrip = cur["strip"], cur["bstrip"]
                    nxt_gen = prologue(h + 1) if h + 1 < 8 else None
                    nxt = None
                    ocs, bocs = ocr.next()
                    tiles = [(qc, kt) for qc in range(NTC) for kt in range(4 * qc + 4)]
                    accs = {}
                    inflight = []

                    def stage1(qc, kt):
                        j = kt - 4 * qc
                        n0 = max(0, j) * 128
                        off = min(qc * 512 - kt * 128 + 384, STRIP_OFFMAX)
                        ps, bps = sring.next()
                        mm(ps[:, n0:512], kT[0:80, kt * 128:(kt + 1) * 128], qT[0:80, qc * 512 + n0:(qc + 1) * 512],
                           True, True, [bk, bq], [bps])
                        pT, bpT = pTr.next()
                        if qc * 512 - kt * 128 >= 1664:
                            S.op("act", lambda: nc.scalar.activation(out=pT[:, :], in_=ps[:, :], func=AF.Exp,
                                                                      bias=b31[:, h:h + 1], scale=SCALE),
                                 reads=[bps, B_b31], writes=[bpT])
                        else:
                            tm, btm = tmr.next()
                            S.op("dve", lambda: nc.vector.scalar_tensor_tensor(
                                out=tm[:, n0:512], in0=ps[:, n0:512], scalar=SCALE, in1=strip[:, off + n0:off + 512],
                                op0=ALU.mult, op1=ALU.add), reads=[bps, bstrip], writes=[btm])
                            S.op("act", lambda: nc.scalar.activation(out=pT[:, n0:512], in_=tm[:, n0:512], func=AF.Exp),
                                 reads=[btm], writes=[bpT])
                        return (qc, kt, n0, pT, bpT)

                    def stage2(qc, kt, n0, pT, bpT):
                        nkt = 4 * qc + 4
                        if kt == 0:
                            accs[qc] = aring.next()
                        pa, bpa = accs[qc]
                        mm(pa[0:65, n0:512], Vc[:, kt, :], pT[:, n0:512], kt == 0, kt == nkt - 1, [bV, bpT], [bpa])
                        if kt == nkt - 1:
                            num, bnum = numr.next()
                            rd, brd = rdr.next()
                            copy("act", num[:, :], pa[0:65, :], [bpa], [bnum])
                            S.op("act", lambda: nc.scalar.activation(out=rd[64:65, :], in_=pa[64:65, :], func=AF.Ln),
                                 reads=[bpa], writes=[brd])
                            S.op("act", lambda: nc.scalar.activation(out=rd[64:65, :], in_=rd[64:65, :], func=AF.Exp, scale=-1.0),
                                 reads=[brd], writes=[brd])

                            def fin_b(qc=qc, num=num, bnum=bnum, rd=rd, brd=brd):
                                pb, bpb = pring.next()
                                mm(pb[0:64, :], ones[64:65, 0:64], rd[64:65, :], True, True, [B_ones, brd], [bpb])
                                S.op("dve", lambda: nc.vector.tensor_tensor(out=ocs[:, qc * 512:(qc + 1) * 512], in0=num[0:64, :],
                                                                             in1=pb[0:64, :], op=ALU.mult),
                                     reads=[bnum, bpb], writes=[bocs], waw=False)
                            deferred.append([10, fin_b])

                    deferred = []
                    for i in range(len(tiles) + LA):
                        if i < len(tiles):
                            inflight.append(stage1(*tiles[i]))
                        if i >= LA:
                            stage2(*inflight.pop(0))
                        for d_ in deferred:
                            d_[0] -= 1
                        while deferred and deferred[0][0] <= 0:
                            deferred.pop(0)[1]()
                        if nxt_gen is not None and i % 3 == 2:
                            try:
                                nxt = next(nxt_gen)
                            except StopIteration:
                                nxt_gen = None
                        if pump_c is not None and i % 7 == 3:
                            pump_c(1)
                    while deferred:
                        deferred.pop(0)[1]()
                    if nxt_gen is not None:
                        r_ = run_all(nxt_gen)
                        nxt = r_ if r_ is not None else nxt
                    S.dma("pool", oT_d[768 + h * 64:768 + (h + 1) * 64, :], ocs[:, :], bocs, reads=[bocs], writes=[B_oT], waw=False)
                    cur = nxt
                if drain_c is not None:
                    drain_c()
                S.barrier()
        if stop_after == "mix":
            return None

        with contextlib.ExitStack() as st:
            R1 = st.enter_context(_sbuf_tensor("R1", [128, NFC * D], BF16))
            B_R1 = Buf("R1")
            Pw_t = st.enter_context(_sbuf_tensor("Pw", [128, 10, D], BF16))
            B_Pw = Buf("Pw")
            Pw = Pw_t[:, :, :]
            S.dma("sp", Pw, pcat_bf[l].rearrange("(k p) m -> p k m", p=128), B_Pw, reads=[Bw], writes=[B_Pw])
            Wo_t = st.enter_context(_sbuf_tensor("Wo", [128, NDC, D], BF16))
            B_Wo = Buf("Wo")
            Wo = Wo_t[:, :, :]
            S.dma("sp", Wo, w_out_bf[l].rearrange("(k p) m -> p k m", p=128), B_Wo, reads=[Bw], writes=[B_Wo])
            Wd = R1[:, :].rearrange("p (k m) -> p k m", k=NFC)
            wd_src = w_down_bf[l].rearrange("(k p) m -> p k m", p=128)
            S.dma("sp", Wd[:, 0:11, :], wd_src[:, 0:11, :], B_R1, reads=[Bw], writes=[B_R1])
            S.dma("sp", Wd[:, 11:22, :], wd_src[:, 11:22, :], B_R1, reads=[Bw], writes=[B_R1], waw=False)
            oTr = Ring([(st.enter_context(_sbuf_tensor("oTt", [128, 10, 512], BF16)), Buf("oTt"))])
            mT = st.enter_context(_sbuf_tensor("mT", [128, NDC, 512], BF16))
            B_mT = Buf("mT")
            gtr = Ring([(st.enter_context(_sbuf_tensor(f"gTt{i}", [128, 3, 512], BF16)), Buf(f"gTt{i}")) for i in range(3)])
            mtr = Ring([(st.enter_context(_sbuf_tensor(f"mtmp{i}", [128, 512], F32)), Buf(f"mtmp{i}")) for i in range(4)])
            xr = Ring([(st.enter_context(_sbuf_tensor(f"fx{i}", [128, D], F32)), Buf(f"fx{i}")) for i in range(2)])
            x1b = st.enter_context(_sbuf_tensor("x1b", [128, 4, D], F32))
            B_x1 = [Buf(f"x1b{i}") for i in range(4)]
            nr = Ring([(st.enter_context(_sbuf_tensor(f"fn{i}", [128, D], BF16)), Buf(f"fn{i}")) for i in range(4)])
            sr = Ring([(st.enter_context(_sbuf_tensor(f"fs{i}", [128, 2], F32)), Buf(f"fs{i}")) for i in range(4)])
            h2T = st.enter_context(_sbuf_tensor("h2T", [128, NDC, 512], BF16))
            B_h2 = Buf("h2T")
            actT = st.enter_context(_sbuf_tensor("actT", [128, NFC, 512], BF16))
            B_act = Buf("actT")
            sgr = Ring([(st.enter_context(_sbuf_tensor(f"sg{i}", [128, 512], F32)), Buf(f"sg{i}")) for i in range(2)])
            x2r = Ring([(st.enter_context(_sbuf_tensor(f"x2s{i}", [128, D], F32)), Buf(f"x2s{i}")) for i in range(2)])
            gfin = None
            if last:
                gfin = st.enter_context(_sbuf_tensor("gfin", [128, D], F32))
                B_gf = Buf("gfin")
                S.dma("sp", gfin[:], gfin_d, B_gf, writes=[B_gf])
            pump_f, drain_f = (None, None)
            print(f"[kernel] FFN phase l={l}: sbuf bytes remaining {nc.sbuf_bytes_remaining}")
            wq_f = WQ([(w_gu_bf[l], off_ + fc_ * 128, 128, Bw) for _tc in range(NTC) for fc_ in range(NFC) for off_ in (0, FFN)], 4)
            def load_oT(tc):
                cs = slice(tc * 512, (tc + 1) * 512)
                oTt, boT = oTr.next()
                S.dma("sp", oTt[:], oT_d[:, cs].rearrange("(k p) t -> p k t", p=128), boT, reads=[B_oT], writes=[boT])
                return oTt, boT

            def load_gT(tc, dc):
                cs = slice(tc * 512, (tc + 1) * 512)
                gt, bgt = gtr.next()
                S.dma("sp", gt[:], gT_d[:, cs].rearrange("(b c p) t -> c p b t", b=3, p=128)[dc], bgt, reads=[B_gT], writes=[bgt])
                return gt, bgt

            def merge(tc, oT_pre):
                oTt, boT = oT_pre
                gq = [load_gT(tc, 0), load_gT(tc, 1)]
                for dc in range(NDC):
                    if dc + 2 < NDC:
                        gq.append(load_gT(tc, dc + 2))
                    gt, bgt = gq.pop(0)
                    tmps = []
                    for bi, (k0, k1) in enumerate(((0, 2), (2, 6), (6, 10))):
                        pb, bpb = allring.next()
                        for k in range(k0, k1):
                            mm(pb[:, :], Pw[:, k, dc * 128:(dc + 1) * 128], oTt[:, k, :], k == k0, k == k1 - 1, [B_Pw, boT], [bpb])
                        tm, btm = mtr.next()
                        S.op("dve", lambda: nc.vector.tensor_tensor(out=tm[:], in0=pb[:, :], in1=gt[:, bi, :], op=ALU.mult),
                             reads=[bpb, bgt], writes=[btm])
                        tmps.append((tm, btm))
                    S.op("pool", lambda: nc.gpsimd.tensor_tensor(out=tmps[0][0][:], in0=tmps[0][0][:], in1=tmps[1][0][:], op=ALU.add),
                         reads=[tmps[0][1], tmps[1][1]], writes=[tmps[0][1]])
                    S.op("pool", lambda: nc.gpsimd.tensor_tensor(out=mT[:, dc, :], in0=tmps[0][0][:], in1=tmps[2][0][:], op=ALU.add),
                         reads=[tmps[0][1], tmps[2][1]], writes=[B_mT], waw=False)

            merge(0, load_oT(0))
            for tc in range(NTC):
                cs = slice(tc * 512, (tc + 1) * 512)
                for i in range(4):
                    t = tc * 4 + i
                    xt, bxt = xr.next()
                    S.dma("sp", xt[:], x_src[t * 128:(t + 1) * 128, :], bxt, reads=[B_xsrc], writes=[bxt])
                    for half in range(2):
                        pb, bpb = allring.next()
                        for dc in range(NDC):
                            mm(pb[:, :], mT[:, dc, i * 128:(i + 1) * 128], Wo[:, dc, half * 512:(half + 1) * 512],
                               dc == 0, dc == NDC - 1, [B_mT, B_Wo], [bpb])
                        S.op("dve", lambda: nc.vector.tensor_tensor(out=x1b[:, i, half * 512:(half + 1) * 512], in0=pb[:, :],
                                                                     in1=xt[:, half * 512:(half + 1) * 512], op=ALU.add),
                             reads=[bpb, bxt], writes=[B_x1[i]], waw=False)
                oT_next = load_oT(tc + 1) if tc + 1 < NTC else None
                nps = []
                for i in range(4):
                    xn, bxn = nr.next()
                    sm, bsm = sr.next()
                    norm_a(x1b[:, i, :], B_x1[i], xn, bxn, sm, bsm)
                    nps.append((xn, bxn, sm, bsm))
                for i in range(4):
                    norm_b(x1b[:, i, :], B_x1[i], nps[i][0], nps[i][1], nps[i][2], nps[i][3])
                for i in range(4):
                    transpose_part(nps[i][0], nps[i][1], h2T, i * 128, B_h2, 8, l)
                for fc in range(NFC):
                    if pump_f is not None:
                        pump_f(1)
                    wg, bwg = wq_f.get()
                    wu, bwu = wq_f.get()
                    pg, bpg = allring.next()
                    for dc in range(NDC):
                        mm(pg[:, :], wg[:, dc, :], h2T[:, dc, :], dc == 0, dc == NDC - 1, [bwg, B_h2], [bpg])
                    pu, bpu = allring.next()
                    for dc in range(NDC):
                        mm(pu[:, :], wu[:, dc, :], h2T[:, dc, :], dc == 0, dc == NDC - 1, [bwu, B_h2], [bpu])
                    sg, bsg = sgr.next()
                    S.op("act", lambda: nc.scalar.activation(out=sg[:], in_=pg[:, :], func=AF.Silu), reads=[bpg], writes=[bsg])
                    S.op("dve", lambda: nc.vector.tensor_tensor(out=actT[:, fc, :], in0=sg[:], in1=pu[:, :], op=ALU.mult),
                         reads=[bsg, bpu], writes=[B_act], waw=False)
                if oT_next is not None:
                    merge(tc + 1, oT_next)
                for i in range(4):
                    t = tc * 4 + i
                    x2, bx2 = x2r.next()
                    for half in range(2):
                        pb, bpb = allring.next()
                        for fc in range(NFC):
                            mm(pb[:, :], actT[:, fc, i * 128:(i + 1) * 128], Wd[:, fc, half * 512:(half + 1) * 512],
                               fc == 0, fc == NFC - 1, [B_act, B_R1], [bpb])
                        S.op("dve", lambda: nc.vector.tensor_tensor(out=x2[:, half * 512:(half + 1) * 512], in0=pb[:, :],
                                                                     in1=x1b[:, i, half * 512:(half + 1) * 512], op=ALU.add),
                             reads=[bpb, B_x1[i]], writes=[bx2], waw=False)
                    if not last:
                        S.dma("pool", x_dst[t * 128:(t + 1) * 128, :], x2[:, :], bx2, reads=[bx2], writes=[B_xdst], waw=False)
                    else:
                        xn, bxn = nr.next()
                        sm, bsm = sr.next()
                        ssq, rstd = sm[:, 0:1], sm[:, 1:2]
                        S.op("act", lambda: nc.scalar.activation(out=xn[:, :], in_=x2[:, :], func=AF.Square, accum_out=ssq),
                             reads=[bx2], writes=[bxn, bsm])
                        S.op("act", lambda: nc.scalar.activation(out=rstd, in_=ssq, func=AF.Sqrt, bias=epsT[:, 0:1], scale=1.0 / D),
                             reads=[bsm, B_vecs], writes=[bsm])
                        S.op("dve", lambda: nc.vector.reciprocal(out=rstd, in_=rstd), reads=[bsm], writes=[bsm])
                        S.op("dve", lambda: nc.vector.scalar_tensor_tensor(out=x2[:, :], in0=x2[:, :], scalar=rstd, in1=gfin[:, :],
                                                                            op0=ALU.mult, op1=ALU.mult),
                             reads=[bx2, bsm, B_gf], writes=[bx2])
                        S.dma("pool", out_d[t * 128:(t + 1) * 128, :], x2[:, :], bx2, reads=[bx2], writes=[B_out], waw=False)
            if drain_f is not None:
                drain_f()
            S.barrier()

    srcs = [(x_in, B_xin), (xs[0], B_xs[0]), (xs[1], B_xs[1])]
    for l in range(n_layers):
        x_src, B_src = srcs[l]
        x_dst, B_dst = srcs[l + 1]
        layer(l, x_src, B_src, x_dst, B_dst, last=(l == n_layers - 1) and not debug)
    S.barrier()
    print(f"[kernel] built program: {S.ninstr} instructions, {S.nsem} semaphores")
    return nc


def _rel_bucket(dist):
    import math
    d = np.maximum(dist, 0)
    df = np.maximum(d, 1).astype(np.float32)
    large = 16 + (np.log(df / np.float32(16)) / np.float32(math.log(2048 / 16)) * np.float32(16)).astype(np.int32)
    large = np.minimum(large, 31)
    return np.where(d < 16, d, large)


def _host_layout(inputs):
    f32 = np.float32
    rel = np.asarray(inputs["rel_bias"], f32)
    ext = np.concatenate([rel, np.full((1, 20), NEG_BIAS, f32)], 0)
    k = np.arange(128)[:, None]
    n = np.arange(256)[None, :]
    delta = n - k
    valid = (delta >= 0) & (delta <= 128)
    biasA = np.empty((128, 12, 256), f32)
    for hd in range(12):
        dil = (1, 4, 16)[hd // 4]
        idx = np.where(valid, _rel_bucket(delta * dil), 32)
        biasA[:, hd, :] = ext[idx, hd]
    c = np.arange(STRIP_W)[None, :]
    dl = c - k - 384
    idx = np.where(dl >= 0, _rel_bucket(dl), 32)
    strip = np.empty((8, 128, STRIP_W), f32)
    for h in range(8):
        strip[h] = ext[idx, 12 + h]
    oh16 = np.zeros((16, S_LEN), ml_dtypes.bfloat16)
    for j in range(16):
        oh16[j, j * 256:(j + 1) * 256] = 1.0
    gmask = np.zeros((128, 2, NT, 16), f32)
    for qt in range(NT):
        gmask[:, 0, qt, qt // 2:] = -1e30
        gmask[:, 1, qt, :qt // 2] = NEGM
    gmask = gmask.reshape(128, 2, 512)
    b31c = np.ascontiguousarray(np.broadcast_to(rel[31:32, 12:20], (128, 8)))
    g_mix = np.asarray(inputs["g_mix"], f32)
    g_ffn = np.asarray(inputs["g_ffn"], f32)
    vecs = np.empty((128, DEPTH, 16), f32)
    vecs[:, :, 0:8] = g_mix.reshape(DEPTH, 8, 128).transpose(2, 0, 1)
    vecs[:, :, 8:16] = g_ffn.reshape(DEPTH, 8, 128).transpose(2, 0, 1)
    lruv = np.empty((128, DEPTH, 4, 8), f32)
    cw = np.asarray(inputs["conv_w"], f32)
    for i in range(4):
        lruv[:, :, :, i] = cw[:, i].reshape(DEPTH, 4, 128).transpose(2, 0, 1)
    for j, key in enumerate(("conv_b", "lru_ba", "lru_bx", "lru_lam")):
        lruv[:, :, :, 4 + j] = np.asarray(inputs[key], f32).reshape(DEPTH, 4, 128).transpose(2, 0, 1)
    lrubd = np.zeros((DEPTH, 2, 128, 4, 128), f32)
    for j, key in enumerate(("lru_wa", "lru_wx")):
        w = np.asarray(inputs[key], f32)
        for cch in range(4):
            for nl in range(2):
                lrubd[:, j, nl * 64:(nl + 1) * 64, cch, nl * 64:(nl + 1) * 64] = w[:, 2 * cch + nl]
    gfin = np.ascontiguousarray(np.broadcast_to(np.asarray(inputs["g_final"], f32)[None, :], (128, D)))
    shared = {
        "w_in": np.ascontiguousarray(inputs["w_in"], f32), "p_a": np.ascontiguousarray(inputs["p_a"], f32),
        "p_b": np.ascontiguousarray(inputs["p_b"], f32), "p_c": np.ascontiguousarray(inputs["p_c"], f32),
        "w_out": np.ascontiguousarray(inputs["w_out"], f32), "w_gu": np.ascontiguousarray(inputs["w_gu"], f32),
        "w_down": np.ascontiguousarray(inputs["w_down"], f32),
        "biasA": biasA, "stripC": strip, "oh16": oh16, "gmask": gmask, "b31c": b31c, "vecs": vecs, "lruv": lruv, "lrubd": lrubd, "g_final_b": gfin,
    }
    return shared


_PROGRAM = {}


def kernel(x, rel_bias, g_mix, w_in, conv_w, conv_b, lru_wa, lru_ba, lru_wx, lru_bx, lru_lam,
           p_a, p_b, p_c, w_out, g_ffn, w_gu, w_down, g_final):
    inputs = dict(x=x, rel_bias=rel_bias, g_mix=g_mix, w_in=w_in, conv_w=conv_w, conv_b=conv_b, lru_wa=lru_wa,
                  lru_ba=lru_ba, lru_wx=lru_wx, lru_bx=lru_bx, lru_lam=lru_lam, p_a=p_a, p_b=p_b, p_c=p_c,
                  w_out=w_out, g_ffn=g_ffn, w_gu=w_gu, w_down=w_down, g_final=g_final)
    shared = _host_layout(inputs)
    xf = np.ascontiguousarray(np.asarray(x, np.float32))
    nb = xf.shape[0]
    nc = build_program()
    in_maps = [dict(shared, x=xf[b]) for b in range(nb)]
    res = run_bass_kernel_spmd(nc, in_maps, core_ids=list(range(nb)))
    return np.stack([np.asarray(r["out"], np.float32) for r in res.results], axis=0)
```

```python
import contextlib
import numpy as np
import ml_dtypes

import concourse.bass as bass
import concourse.mybir as mybir
from concourse.bass_utils import run_bass_kernel_spmd

F32 = mybir.dt.float32
BF16 = mybir.dt.bfloat16
AF = mybir.ActivationFunctionType
ALU = mybir.AluOpType
AX = mybir.AxisListType

S_LEN = 4096
D = 1024
NT = 32
NDC = 8
NTC = 8
DEPTH = 2
IN_COLS = 7936
QA, KA, VA, XB, GB, QC, KC, VC, GT = 0, 768, 1536, 2304, 2816, 3328, 3840, 4352, 4864
FFN = 2816
NFC = 22
SCALE = 0.125
EPS = 1e-6
NEGM = -262144.0
NEG_BIAS = -30000.0
STRIP_W = 2560
STRIP_OFFMAX = 2048
SEM_LIMIT = 30000


class Buf:
    __slots__ = ("name", "writers", "readers", "dsem", "dcnt")

    def __init__(self, name):
        self.name = name
        self.writers = {}
        self.readers = {}
        self.dsem = None
        self.dcnt = 0


class Sched:
    def __init__(self, nc):
        self.nc = nc
        self.eng = {"pe": nc.tensor, "act": nc.scalar, "dve": nc.vector, "pool": nc.gpsimd, "sp": nc.sync}
        self.sem = {}
        self.cnt = {}
        self.seen = {e: {} for e in self.eng}
        self.nsem = 0
        self.dbufs = []
        self.free_dsems = []
        for e in self.eng:
            self._new_sem(e)
        self.ninstr = 0

    def _alloc(self, name):
        self.nsem += 1
        return self.nc.alloc_semaphore(f"{name}_{self.nsem}")

    def _new_sem(self, e):
        self.sem[e] = self._alloc("s_" + e)
        self.cnt[e] = 0

    def _wait(self, e, deps):
        seen = self.seen[e]
        own = self.sem[e]
        for s, v in deps.items():
            if s is own and e == "pe":
                continue
            if seen.get(s, 0) >= v:
                continue
            self.eng[e].wait_ge(s, v)
            self.ninstr += 1
            seen[s] = v

    @staticmethod
    def _collect(reads, writes, waw):
        deps = {}
        for b in reads:
            for s, v in b.writers.items():
                if deps.get(s, 0) < v:
                    deps[s] = v
        for b in writes:
            if waw:
                for s, v in b.writers.items():
                    if deps.get(s, 0) < v:
                        deps[s] = v
            for s, v in b.readers.items():
                if deps.get(s, 0) < v:
                    deps[s] = v
        return deps

    @staticmethod
    def _record(s, v, reads, writes, waw):
        for b in reads:
            if b.readers.get(s, 0) < v:
                b.readers[s] = v
        for b in writes:
            if waw:
                b.writers = {s: v}
            else:
                b.writers[s] = v
            b.readers = {}

    def op(self, e, fn, reads=(), writes=(), waw=True):
        self._wait(e, self._collect(reads, writes, waw))
        if self.cnt[e] >= SEM_LIMIT:
            self._new_sem(e)
        ins = fn()
        self.cnt[e] += 1
        s, v = self.sem[e], self.cnt[e]
        ins.then_inc(s, 1)
        self.ninstr += 1
        self._record(s, v, reads, writes, waw)
        return ins

    def dma(self, e, out, in_, sbuf, reads=(), writes=(), waw=True):
        self._wait(e, self._collect(reads, writes, waw))
        if sbuf.dsem is None or sbuf.dcnt >= SEM_LIMIT:
            if sbuf.dsem is not None and sbuf in self.dbufs:
                self.dbufs.remove(sbuf)
            if self.free_dsems:
                sbuf.dsem, sbuf.dcnt = self.free_dsems.pop()
            else:
                sbuf.dsem = self._alloc("d_" + sbuf.name)
                sbuf.dcnt = 0
            self.dbufs.append(sbuf)
        ins = self.eng[e].dma_start(out=out, in_=in_)
        sbuf.dcnt += 16
        s, v = sbuf.dsem, sbuf.dcnt
        ins.then_inc(s, 16)
        self.ninstr += 1
        self._record(s, v, reads, writes, waw)
        return ins

    def barrier(self):
        deps = {}
        for e in self.eng:
            if self.cnt[e] > 0:
                deps[self.sem[e]] = self.cnt[e]
        for b in self.dbufs:
            if b.dsem is not None and b.dcnt > 0:
                deps[b.dsem] = b.dcnt
        for e in self.eng:
            seen = self.seen[e]
            for s, v in deps.items():
                if seen.get(s, 0) >= v:
                    continue
                self.eng[e].wait_ge(s, v)
                self.ninstr += 1
                seen[s] = v
        for b in self.dbufs:
            if b.dsem is not None and b.dcnt < SEM_LIMIT - 4000:
                self.free_dsems.append((b.dsem, b.dcnt))
            b.dsem = None
            b.dcnt = 0
        self.dbufs = []


def ssl(s0, n, d):
    return slice(s0, s0 + (n - 1) * d + 1, d)


class Ring:
    def __init__(self, items):
        self.items = items
        self.i = 0

    def next(self):
        it = self.items[self.i % len(self.items)]
        self.i += 1
        return it


def build_program(n_layers=DEPTH, debug=False, stop_after=None):
    nc = bass.Bass("TRN2", target_bir_lowering=False)
    S = Sched(nc)

    _uid = [0]
    _orig_sbuf_tensor = nc.sbuf_tensor

    def _sbuf_tensor(name, shape, dt):
        _uid[0] += 1
        return _orig_sbuf_tensor(f"{name}_u{_uid[0]}", shape, dt)

    def din(name, shape, dt=F32):
        return nc.dram_tensor(name, list(shape), dt, kind="ExternalInput").ap()

    def dscr(name, shape, dt):
        kind = "ExternalOutput" if debug else "Internal"
        return nc.dram_tensor(name, list(shape), dt, kind=kind).ap()

    x_in = din("x", [S_LEN, D])
    w_in = din("w_in", [DEPTH, D, IN_COLS])
    p_a = din("p_a", [DEPTH, 256, D])
    p_b = din("p_b", [DEPTH, 512, D])
    p_c = din("p_c", [DEPTH, 512, D])
    w_out = din("w_out", [DEPTH, D, D])
    w_gu = din("w_gu", [DEPTH, D, 2 * FFN])
    w_down = din("w_down", [DEPTH, FFN, D])
    biasA_d = din("biasA", [128, 12, 256])
    strip_d = din("stripC", [8, 128, STRIP_W])
    oh16_d = din("oh16", [16, S_LEN], BF16)
    gmask_d = din("gmask", [128, 2, 512])
    b31_d = din("b31c", [128, 8])
    vec_d = din("vecs", [128, DEPTH, 16])
    lruv_d = din("lruv", [128, DEPTH, 4, 8])
    lrubd_d = din("lrubd", [DEPTH, 2, 128, 4, 128])
    gfin_d = din("g_final_b", [128, D])
    out_d = nc.dram_tensor("out", [S_LEN, D], F32, kind="ExternalOutput").ap()

    w_in_bf = dscr("w_in_bf", [DEPTH, D, IN_COLS], BF16)
    pcat_bf = dscr("pcat_bf", [DEPTH, 1280, D], BF16)
    w_out_bf = dscr("w_out_bf", [DEPTH, D, D], BF16)
    w_gu_bf = dscr("w_gu_bf", [DEPTH, D, 2 * FFN], BF16)
    w_down_bf = dscr("w_down_bf", [DEPTH, FFN, D], BF16)
    xs = [dscr("xs0", [S_LEN, D], F32), dscr("xs1", [S_LEN, D], F32)]
    oT_d = dscr("oT", [1280, S_LEN], BF16)
    gT_d = dscr("gT", [3 * D, S_LEN], BF16)

    B_wbf = [Buf(f"wbf{l}") for l in range(DEPTH)]
    B_xs = [Buf("xs0"), Buf("xs1")]
    B_oT = Buf("oT")
    B_gT = Buf("gT")
    B_xin = Buf("xin")
    B_out = Buf("out")
    B_const = Buf("const_in")

    ident = nc.alloc_sbuf_tensor("ident", [128, 128], BF16)
    B_ident = Buf("ident")
    ones = nc.alloc_sbuf_tensor("ones", [128, 64], F32)
    B_ones = Buf("ones")
    vecs = nc.alloc_sbuf_tensor("vecs_sb", [128, DEPTH, 16], F32)
    B_vecs = Buf("vecs")
    epsT = nc.alloc_sbuf_tensor("epsT", [128, 1], F32)
    NW = 6
    wring = Ring([(nc.alloc_sbuf_tensor(f"wslot{i}", [128, NDC, 128], BF16), Buf(f"wslot{i}")) for i in range(NW)])
    pbanks = [(nc.alloc_psum_tensor(f"pb{i}", [128, 512], F32), Buf(f"pb{i}")) for i in range(8)]
    ptr = pbanks[7][0][:, :].bitcast(BF16)
    B_ptr = pbanks[7][1]
    pring = Ring(pbanks[0:2])
    sring = Ring(pbanks[2:6])
    aring = Ring(pbanks[6:8])
    allring = Ring(pbanks[0:6])

    S.op("pool", lambda: nc.gpsimd.memset(ident[:], 1.0), writes=[B_ident])
    S.op("pool", lambda: nc.gpsimd.affine_select(out=ident[:], in_=ident[:], pattern=[[-1, 128]],
                                                  compare_op=ALU.is_equal, fill=0.0, base=0, channel_multiplier=1),
         reads=[B_ident], writes=[B_ident])
    S.op("pool", lambda: nc.gpsimd.memset(ones[:], 1.0), writes=[B_ones])
    S.op("pool", lambda: nc.gpsimd.memset(epsT[:], EPS), writes=[B_vecs])
    S.dma("sp", vecs[:], vec_d, B_vecs, writes=[B_vecs], waw=False)

    ev_i = [0]

    def evac_eng():
        ev_i[0] += 1
        return "act" if ev_i[0] % 2 else "dve"

    def copy(e, out, in_, reads, writes, waw=True):
        if e == "act":
            return S.op("act", lambda: nc.scalar.copy(out=out, in_=in_), reads, writes, waw)
        if e == "dve":
            return S.op("dve", lambda: nc.vector.tensor_copy(out=out, in_=in_), reads, writes, waw)
        return S.op("pool", lambda: nc.gpsimd.tensor_copy(out=out, in_=in_), reads, writes, waw)

    def mm(out, lhsT, rhs, start, stop, reads, writes):
        return S.op("pe", lambda: nc.tensor.matmul(out, lhsT=lhsT, rhs=rhs, start=start, stop=stop), reads, writes)

    def load_w(src2d, col0, ncols, B_src):
        wt, bw = wring.next()
        S.dma("sp", wt[:, :, 0:ncols], src2d[:, col0:col0 + ncols].rearrange("(dc p) m -> p dc m", p=128),
              bw, reads=[B_src], writes=[bw])
        return wt, bw

    class WQ:
        def __init__(self, specs, depth):
            self.specs = list(specs)
            self.depth = depth
            self.i = 0
            self.q = []

        def get(self):
            while len(self.q) < self.depth + 1 and self.i < len(self.specs):
                src2d, col0, n, B_src = self.specs[self.i]
                self.i += 1
                self.q.append(load_w(src2d, col0, n, B_src))
            return self.q.pop(0)

    CW = 1024

    def tiles2d(src, dst, rows, cols):
        out = []
        for r0 in range(0, rows, 128):
            for c0 in range(0, cols, CW):
                n = min(CW, cols - c0)
                out.append((src[r0:r0 + 128, c0:c0 + n], dst[r0:r0 + 128, c0:c0 + n], n))
        return out

    def weight_tiles(l, with_w_in=True, with_rest=True):
        t = []
        if with_w_in:
            t += tiles2d(w_in[l], w_in_bf[l], D, IN_COLS)
        if with_rest:
            t += tiles2d(p_a[l], pcat_bf[l, 0:256], 256, D)
            t += tiles2d(p_b[l], pcat_bf[l, 256:768], 512, D)
            t += tiles2d(p_c[l], pcat_bf[l, 768:1280], 512, D)
            t += tiles2d(w_out[l], w_out_bf[l], D, D)
            t += tiles2d(w_gu[l], w_gu_bf[l], D, 2 * FFN)
            t += tiles2d(w_down[l], w_down_bf[l], FFN, D)
        return t

    def make_pump(st, tiles, B_dst, cast_engs, store_q, nbuf=3):
        stg = Ring([(st.enter_context(_sbuf_tensor(f"cst{i}", [128, CW], F32)), Buf(f"cst{i}")) for i in range(nbuf)])
        stb = Ring([(st.enter_context(_sbuf_tensor(f"csb{i}", [128, CW], BF16)), Buf(f"csb{i}")) for i in range(nbuf)])
        ce = Ring(list(cast_engs))
        todo = list(tiles)
        loaded = []

        def load_one():
            src, dst, n = todo.pop(0)
            a_, ba = stg.next()
            S.dma("sp", a_[:, 0:n], src, ba, writes=[ba])
            loaded.append((a_, ba, dst, n))

        def pump(k):
            for _ in range(k):
                if not todo and not loaded:
                    return
                while todo and len(loaded) < nbuf - 1:
                    load_one()
                a_, ba, dst, n = loaded.pop(0)
                b_, bb = stb.next()
                copy(ce.next(), b_[:, 0:n], a_[:, 0:n], [ba], [bb])
                S.dma(store_q, dst, b_[:, 0:n], bb, reads=[bb], writes=[B_dst], waw=False)
                if todo:
                    load_one()

        def drain():
            while todo or loaded:
                pump(1)
        return pump, drain

    def phase_cast():
        with contextlib.ExitStack() as st:
            _, drain = make_pump(st, weight_tiles(0, True, False), B_wbf[0], ["pool", "dve", "act"], "act", nbuf=4)
            drain()
            S.barrier()

    tbanks = [(pbanks[7][0][:, :].bitcast(BF16), pbanks[7][1]), (pbanks[6][0][:, :].bitcast(BF16), pbanks[6][1])]
    tb_i = [0]

    def norm_a(xt, bxt, xn, bxn, small, bsmall):
        ssq = small[:, 0:1]
        rstd = small[:, 1:2]
        S.op("act", lambda: nc.scalar.activation(out=xn[:, :], in_=xt, func=AF.Square, accum_out=ssq),
             reads=[bxt], writes=[bxn, bsmall])
        S.op("act", lambda: nc.scalar.activation(out=rstd, in_=ssq, func=AF.Sqrt, bias=epsT[:, 0:1], scale=1.0 / D),
             reads=[bsmall, B_vecs], writes=[bsmall])
        S.op("dve", lambda: nc.vector.reciprocal(out=rstd, in_=rstd), reads=[bsmall], writes=[bsmall])

    def norm_b(xt, bxt, xn, bxn, small, bsmall):
        S.op("act", lambda: nc.scalar.activation(out=xn[:, :], in_=xt, func=AF.Copy, scale=small[:, 1:2]),
             reads=[bxt, bsmall], writes=[bxn])

    def transpose_part(xn, bxn, dstT, col0, B_dst, gcol, l):
        tb_i[0] += 1
        tp, B_tp = tbanks[tb_i[0] % 2]
        for dc in range(NDC):
            S.op("pe", lambda dc=dc: nc.tensor.transpose(out=tp[:, dc * 128:(dc + 1) * 128],
                                                        in_=xn[:, dc * 128:(dc + 1) * 128], identity=ident[:]),
                 reads=[bxn, B_ident], writes=[B_tp])
        g_b = vecs[:, l, gcol:gcol + 8].unsqueeze(2).to_broadcast([128, NDC, 128])
        S.op("dve", lambda: nc.vector.tensor_tensor(out=dstT[:, :, col0:col0 + 128],
                                                     in0=tp.rearrange("p (c t) -> p c t", c=NDC),
                                                     in1=g_b, op=ALU.mult),
             reads=[B_tp, B_vecs], writes=[B_dst], waw=False)

    def layer(l, x_src, B_xsrc, x_dst, B_xdst, last):
        Bw = B_wbf[l]
        rest0 = weight_tiles(0, False, True) if l == 0 else []
        next_tiles = weight_tiles(l + 1, True, True) if l + 1 < n_layers else []
        with contextlib.ExitStack() as att:
            hT = att.enter_context(_sbuf_tensor("hT", [128, NDC, S_LEN], BF16))
            B_hT = Buf("hT")

            with contextlib.ExitStack() as st:
                xr = Ring([(st.enter_context(_sbuf_tensor(f"p1x{i}", [128, D], F32)), Buf(f"p1x{i}")) for i in range(4)])
                nr = Ring([(st.enter_context(_sbuf_tensor(f"p1n{i}", [128, D], BF16)), Buf(f"p1n{i}")) for i in range(4)])
                sr = Ring([(st.enter_context(_sbuf_tensor(f"p1s{i}", [128, 2], F32)), Buf(f"p1s{i}")) for i in range(4)])
                pend = []
                pend2 = []
                pump_i, drain_i = (None, None)
                if l == 0:
                    pump_i, drain_i = make_pump(st, weight_tiles(0, True, False), B_wbf[0], ["dve", "act"], "pool")
                items = {}
                for t in range(NT + 2):
                    if pump_i is not None:
                        pump_i(2)
                    if t < NT:
                        xt, bxt = xr.next()
                        xn, bxn = nr.next()
                        sm, bsm = sr.next()
                        S.dma("sp", xt[:], x_src[t * 128:(t + 1) * 128, :], bxt, reads=[B_xsrc], writes=[bxt])
                        norm_a(xt[:, :], bxt, xn, bxn, sm, bsm)
                        items[t] = (xt, bxt, xn, bxn, sm, bsm)
                    if 0 <= t - 1 < NT:
                        xt_, bxt_, xn_, bxn_, sm_, bsm_ = items[t - 1]
                        norm_b(xt_[:, :], bxt_, xn_, bxn_, sm_, bsm_)
                    if 0 <= t - 2 < NT:
                        xt_, bxt_, xn_, bxn_, sm_, bsm_ = items.pop(t - 2)
                        transpose_part(xn_, bxn_, hT, (t - 2) * 128, B_hT, 0, l)
                if drain_i is not None:
                    drain_i()
                S.barrier()
            if stop_after == "p1":
                return hT

            def proj_fm(col0, M, dst_fn, dst_bufs, func=None, e_fixed=None, ntc=NTC, tc0=0, ring=None, w=None):
                wt, bw = w if w is not None else load_w(w_in_bf[l], col0, M, Bw)
                for tc in range(tc0, tc0 + ntc):
                    pb, bpb = (ring or pring).next()
                    for dc in range(NDC):
                        mm(pb[0:M, :], wt[:, dc, 0:M], hT[:, dc, tc * 512:(tc + 1) * 512], dc == 0, dc == NDC - 1,
                           [bw, B_hT], [bpb])
                    dst = dst_fn(tc)
                    if func is not None:
                        S.op("act", lambda: nc.scalar.activation(out=dst, in_=pb[0:M, :], func=func),
                             reads=[bpb], writes=dst_bufs, waw=False)
                    else:
                        copy(e_fixed or evac_eng(), dst, pb[0:M, :], [bpb], dst_bufs, waw=False)

            with contextlib.ExitStack() as st:
                gr = Ring([(st.enter_context(_sbuf_tensor(f"pgs{i}", [128, S_LEN], BF16)), Buf(f"pgs{i}")) for i in range(2)])
                pump_g, drain_g = (None, None)
                if l == 0:
                    pump_g, drain_g = make_pump(st, rest0[:24], B_wbf[0], ["pool"], "pool")
                wq_g = WQ([(w_in_bf[l], GT + gi * 128, 128, Bw) for gi in range(24)], 2)
                for gi in range(24):
                    if pump_g is not None:
                        pump_g(1)
                    gs, bgs = gr.next()
                    proj_fm(GT + gi * 128, 128, lambda tc: gs[:, tc * 512:(tc + 1) * 512], [bgs], func=AF.Sigmoid, ring=allring,
                            w=wq_g.get())
                    S.dma("sp", gT_d[gi * 128:(gi + 1) * 128, :], gs[:, :], bgs, reads=[bgs], writes=[B_gT], waw=False)
                if drain_g is not None:
                    drain_g()
                S.barrier()

            with contextlib.ExitStack() as st:
                biasA = st.enter_context(_sbuf_tensor("biasA_sb", [128, 12, 256], F32))
                B_bA = Buf("biasA")
                S.dma("sp", biasA[:], biasA_d, B_bA, writes=[B_bA])
                qTa = st.enter_context(_sbuf_tensor("qTa", [128, S_LEN], BF16))
                kTa = st.enter_context(_sbuf_tensor("kTa", [128, S_LEN], BF16))
                B_qTa, B_kTa = Buf("qTa"), Buf("kTa")
                Va = st.enter_context(_sbuf_tensor("Va", [128, NT, 2, 65], BF16))
                B_Va = Buf("Va")
                acc = st.enter_context(_sbuf_tensor("accA", [65, 2, S_LEN], F32))
                B_acc = Buf("accA")
                str_ = Ring([(st.enter_context(_sbuf_tensor(f"sTa{i}", [128, 256], F32)), Buf(f"sTa{i}")) for i in range(5)])
                ptr_ = Ring([(st.enter_context(_sbuf_tensor(f"pTa{i}", [128, 256], BF16)), Buf(f"pTa{i}")) for i in range(7)])
                rden = st.enter_context(_sbuf_tensor("rdenA", [65, S_LEN], F32))
                B_rden = Buf("rdenA")
                oas = st.enter_context(_sbuf_tensor("oas", [64, S_LEN], BF16))
                B_oas = Buf("oas")
                S.op("pool", lambda: nc.gpsimd.memset(Va[:, :, :, 64:65], 1.0), writes=[B_Va])
                pump_a, drain_a = (None, None)
                if l == 0:
                    pump_a, drain_a = make_pump(st, rest0[24:], B_wbf[0], ["pool"], "pool")
                wq_a = WQ([(w_in_bf[l], base + g_ * 256 + p_ * 128, 128, Bw) for p_ in range(2) for g_ in range(3)
                           for base in (QA, KA, VA)], 2)
                for p in range(2):
                    for g, d in enumerate((1, 4, 16)):
                        fo = g * 256 + p * 128
                        nb = NT // d
                        proj_fm(QA + fo, 128, lambda tc: qTa[:, tc * 512:(tc + 1) * 512], [B_qTa], w=wq_a.get())
                        proj_fm(KA + fo, 128, lambda tc: kTa[:, tc * 512:(tc + 1) * 512], [B_kTa], w=wq_a.get())
                        wv, bwv = wq_a.get()
                        for kt0 in range(0, NT, 4):
                            pb, bpb = pring.next()
                            for i in range(4):
                                kt = kt0 + i
                                r, m = kt // nb, kt % nb
                                s0 = m * 128 * d + r
                                for dc in range(NDC):
                                    mm(pb[:, i * 128:(i + 1) * 128], hT[:, dc, ssl(s0, 128, d)], wv[:, dc, :],
                                       dc == 0, dc == NDC - 1, [bwv, B_hT], [bpb])
                            copy(evac_eng(), Va[:, kt0:kt0 + 4, :, 0:64],
                                 pb[:, :].rearrange("p (k h e) -> p k h e", k=4, h=2), [bpb], [B_Va], waw=False)
                        items = [(hh, r, m) for hh in range(2) for r in range(d) for m in range(nb)]
                        nbb = min(4, nb)
                        stt = {}

                        def a_stage1(hh, r, m):
                            hd = g * 4 + p * 2 + hh
                            hs = slice(hh * 64, (hh + 1) * 64)
                            nq = 256 if m < nb - 1 else 128
                            s0 = m * 128 * d + r
                            ps, bps = sring.next()
                            mm(ps[:, 0:nq], kTa[hs, ssl(s0, 128, d)], qTa[hs, ssl(s0, nq, d)], True, True,
                               [B_kTa, B_qTa], [bps])
                            sT, bsT = str_.next()
                            S.op("dve", lambda: nc.vector.scalar_tensor_tensor(
                                out=sT[:, 0:nq], in0=ps[:, 0:nq], scalar=SCALE, in1=biasA[:, hd, 0:nq],
                                op0=ALU.mult, op1=ALU.add), reads=[bps, B_bA], writes=[bsT])
                            pT, bpT = ptr_.next()
                            S.op("act", lambda: nc.scalar.activation(out=pT[:, 0:nq], in_=sT[:, 0:nq], func=AF.Exp),
                                 reads=[bsT], writes=[bpT])
                            return (hh, r, m, pT, bpT)

                        def a_stage2(hh, r, m, pT, bpT):
                            if m % nbb == 0:
                                stt["pacc"] = aring.next()
                            pa, bpa = stt["pacc"]
                            cs = slice((m % nbb) * 128, (m % nbb + 1) * 128)
                            kt = r * nb + m
                            if m > 0:
                                ppT, bppT = stt["prev"]
                                mm(pa[0:65, cs], Va[:, kt - 1, hh, :], ppT[:, 128:256], True, False, [B_Va, bppT], [bpa])
                                mm(pa[0:65, cs], Va[:, kt, hh, :], pT[:, 0:128], False, True, [B_Va, bpT], [bpa])
                            else:
                                mm(pa[0:65, cs], Va[:, kt, hh, :], pT[:, 0:128], True, True, [B_Va, bpT], [bpa])
                            stt["prev"] = (pT, bpT)
                            if m % nbb == nbb - 1:
                                m0 = m - (nbb - 1)
                                t0 = m0 * 128 * d + r
                                acc_ap = acc[:, hh, ssl(t0, nbb * 128, d)]
                                if g == 0:
                                    copy("act", acc_ap, pa[0:65, 0:nbb * 128], [bpa], [B_acc], waw=False)
                                else:
                                    S.op("dve", lambda: nc.vector.tensor_tensor(
                                        out=acc_ap, in0=acc_ap, in1=pa[0:65, 0:nbb * 128], op=ALU.add),
                                        reads=[bpa, B_acc], writes=[B_acc], waw=False)

                        LA_A = 3
                        infl = []
                        for i in range(len(items) + LA_A):
                            if i < len(items):
                                infl.append(a_stage1(*items[i]))
                            if i >= LA_A:
                                a_stage2(*infl.pop(0))
                            if pump_a is not None and i % 6 == 3:
                                pump_a(1)
                    for hh in range(2):
                        S.op("act", lambda: nc.scalar.activation(out=rden[64:65, :], in_=acc[64:65, hh, :], func=AF.Ln),
                             reads=[B_acc], writes=[B_rden])
                        S.op("act", lambda: nc.scalar.activation(out=rden[64:65, :], in_=rden[64:65, :], func=AF.Exp, scale=-1.0),
                             reads=[B_rden], writes=[B_rden])
                        for tc in range(NTC):
                            cs = slice(tc * 512, (tc + 1) * 512)
                            pb, bpb = pring.next()
                            mm(pb[0:64, :], ones[64:65, 0:64], rden[64:65, cs], True, True, [B_ones, B_rden], [bpb])
                            S.op("dve", lambda: nc.vector.tensor_tensor(out=oas[:, cs], in0=acc[0:64, hh, cs], in1=pb[0:64, :],
                                                                         op=ALU.mult),
                                 reads=[bpb, B_acc], writes=[B_oas], waw=False)
                        slot = p * 2 + hh
                        S.dma("pool", oT_d[slot * 64:(slot + 1) * 64, :], oas[:, :], B_oas, reads=[B_oas], writes=[B_oT], waw=False)
                if drain_a is not None:
                    drain_a()
                S.barrier()

            with contextlib.ExitStack() as st:
                TB = 512
                lruv = st.enter_context(_sbuf_tensor("lruv", [128, 4, 8], F32))
                B_lv = Buf("lruv")
                S.dma("sp", lruv[:], lruv_d[:, l], B_lv, writes=[B_lv])
                bdf = st.enter_context(_sbuf_tensor("bdf", [128, 2, 4, 128], F32))
                bdb = st.enter_context(_sbuf_tensor("bdb", [128, 2, 4, 128], BF16))
                B_bd = Buf("bd")
                for j in range(2):
                    S.dma("sp", bdf[:, j], lrubd_d[l, j], B_bd, writes=[B_bd], waw=False)
                copy("dve", bdb[:], bdf[:], [B_bd], [B_bd])
                cl = st.enter_context(_sbuf_tensor("cl", [128, 4], F32))
                S.op("act", lambda: nc.scalar.activation(out=cl[:], in_=lruv[:, :, 7], func=AF.Exp, scale=-1.0),
                     reads=[B_lv], writes=[B_lv])
                S.op("act", lambda: nc.scalar.activation(out=cl[:], in_=cl[:], func=AF.Ln, bias=1.0, scale=1.0),
                     reads=[B_lv], writes=[B_lv])
                S.op("dve", lambda: nc.vector.tensor_scalar(out=cl[:], in0=cl[:], scalar1=-8.0, scalar2=None, op0=ALU.mult),
                     reads=[B_lv], writes=[B_lv])
                cl2 = st.enter_context(_sbuf_tensor("cl2", [128, 4], F32))
                S.op("dve", lambda: nc.vector.tensor_scalar(out=cl2[:], in0=cl[:], scalar1=2.0, scalar2=None, op0=ALU.mult),
                     reads=[B_lv], writes=[B_lv])
                wxg = st.enter_context(_sbuf_tensor("wxg", [128, 8, NDC, 128], BF16))
                B_wxg = [Buf(f"wxg{i}") for i in range(8)]
                for c in range(4):
                    for j, col in enumerate((XB, GB)):
                        S.dma("sp", wxg[:, j * 4 + c], w_in_bf[l][:, col + c * 128:col + (c + 1) * 128].rearrange("(dc p) m -> p dc m", p=128),
                              B_wxg[j * 4 + c], reads=[Bw], writes=[B_wxg[j * 4 + c]])

                def tl(name, dt=F32, w=TB):
                    return [(st.enter_context(_sbuf_tensor(f"{name}{c}", [128, w], dt)), Buf(f"{name}{c}")) for c in range(4)]
                XB2 = [tl("xbh0_", F32, TB + 3), tl("xbh1_", F32, TB + 3)]
                GB2 = [tl("gb0_"), tl("gb1_")]
                XC, XCB, RR, IG, AA, HH = tl("xc"), tl("xcb", BF16), tl("r"), tl("ig"), tl("a"), tl("hh")
                UU, OB = tl("u"), tl("ob", BF16)
                A2 = tl("a2")
                BT = A2
                print(f"[kernel] PB phase l={l}: sbuf bytes remaining {nc.sbuf_bytes_remaining}")
                HALO, HCAR = tl("halo", F32, 3), tl("hcar", F32, 1)
                CH = range(4)
                def s1(tb):
                    tcs = slice(tb * TB, (tb + 1) * TB)
                    for c in CH:
                        xb, bxb = XB2[tb % 2][c]
                        gb, bgb = GB2[tb % 2][c]
                        pb, bpb = allring.next()
                        for dc in range(NDC):
                            mm(pb[:, :], wxg[:, c, dc, :], hT[:, dc, tcs], dc == 0, dc == NDC - 1, [B_wxg[c], B_hT], [bpb])
                        copy("act", xb[:, 3:3 + TB], pb[:, :], [bpb], [bxb], waw=False)
                        pb, bpb = allring.next()
                        for dc in range(NDC):
                            mm(pb[:, :], wxg[:, 4 + c, dc, :], hT[:, dc, tcs], dc == 0, dc == NDC - 1, [B_wxg[4 + c], B_hT], [bpb])
                        copy("dve", gb[:, :], pb[:, :], [bpb], [bgb])

                s1(0)
                for tb in range(S_LEN // TB):
                    tcs = slice(tb * TB, (tb + 1) * TB)
                    XBt = XB2[tb % 2]
                    GBt = GB2[tb % 2]
                    if tb + 1 < S_LEN // TB:
                        s1(tb + 1)
                    for c in CH:
                        xb, bxb = XBt[c]
                        if tb == 0:
                            S.op("pool", lambda: nc.gpsimd.memset(xb[:, 0:3], 0.0), writes=[bxb], waw=False)
                        else:
                            copy("pool", xb[:, 0:3], HALO[c][0][:, :], [HALO[c][1]], [bxb], waw=False)
                    for c in CH:
                        xb, bxb = XBt[c]
                        xc, bxc = XC[c]
                        S.op("dve", lambda: nc.vector.tensor_scalar(out=xc[:], in0=xb[:, 0:TB], scalar1=lruv[:, c, 0:1],
                                                                     scalar2=lruv[:, c, 4:5], op0=ALU.mult, op1=ALU.add),
                             reads=[bxb, B_lv], writes=[bxc])
                    for i in range(1, 4):
                        for c in CH:
                            xb, bxb = XBt[c]
                            xc, bxc = XC[c]
                            S.op("dve", lambda: nc.vector.scalar_tensor_tensor(out=xc[:], in0=xb[:, i:i + TB],
                                                                                scalar=lruv[:, c, i:i + 1], in1=xc[:],
                                                                                op0=ALU.mult, op1=ALU.add),
                                 reads=[bxb, B_lv, bxc], writes=[bxc])
                    for c in CH:
                        copy("pool", HALO[c][0][:, :], XBt[c][0][:, TB:TB + 3], [XBt[c][1]], [HALO[c][1]])
                        copy("act", XCB[c][0][:], XC[c][0][:], [XC[c][1]], [XCB[c][1]])
                    for c in CH:
                        xcb, bxcb = XCB[c]
                        r_, br_ = RR[c]
                        ig, big = IG[c]
                        pb, bpb = allring.next()
                        mm(pb[:, :], bdb[:, 0, c, :], xcb[:, :], True, True, [B_bd, bxcb], [bpb])
                        S.op("act", lambda: nc.scalar.activation(out=r_[:, :], in_=pb[:, :], func=AF.Sigmoid, bias=lruv[:, c, 5:6]),
                             reads=[bpb, B_lv], writes=[br_])
                        pb, bpb = allring.next()
                        mm(pb[:, :], bdb[:, 1, c, :], xcb[:, :], True, True, [B_bd, bxcb], [bpb])
                        S.op("act", lambda: nc.scalar.activation(out=ig[:, :], in_=pb[:, :], func=AF.Sigmoid, bias=lruv[:, c, 6:7]),
                             reads=[bpb, B_lv], writes=[big])
                    for c in CH:
                        S.op("act", lambda: nc.scalar.activation(out=AA[c][0][:], in_=RR[c][0][:], func=AF.Exp, scale=cl[:, c:c + 1]),
                             reads=[RR[c][1], B_lv], writes=[AA[c][1]])
                    for c in CH:
                        S.op("act", lambda: nc.scalar.activation(out=A2[c][0][:], in_=RR[c][0][:], func=AF.Exp, scale=cl2[:, c:c + 1]),
                             reads=[RR[c][1], B_lv], writes=[A2[c][1]])
                    for c in CH:
                        S.op("act", lambda: nc.scalar.activation(out=RR[c][0][:], in_=A2[c][0][:], func=AF.Sqrt, bias=1.0, scale=-1.0),
                             reads=[A2[c][1], RR[c][1]], writes=[RR[c][1]])
                    for c in CH:
                        S.op("pool", lambda: nc.gpsimd.tensor_tensor(out=IG[c][0][:], in0=IG[c][0][:], in1=XC[c][0][:], op=ALU.mult),
                             reads=[IG[c][1], XC[c][1]], writes=[IG[c][1]])
                    for c in CH:
                        S.op("dve", lambda: nc.vector.tensor_tensor(out=BT[c][0][:], in0=IG[c][0][:], in1=RR[c][0][:], op=ALU.mult),
                             reads=[IG[c][1], RR[c][1]], writes=[BT[c][1]])
                    for c in CH:
                        init = 0.0 if tb == 0 else HCAR[c][0][:, 0:1]
                        rd = [AA[c][1], BT[c][1]] + ([] if tb == 0 else [HCAR[c][1]])
                        S.op("dve", lambda: nc.vector.tensor_tensor_scan(out=HH[c][0][:], data0=AA[c][0][:], data1=BT[c][0][:],
                                                                          initial=init, op0=ALU.mult, op1=ALU.add),
                             reads=rd, writes=[HH[c][1]])
                    for c in CH:
                        copy("act", HCAR[c][0][:, 0:1], HH[c][0][:, TB - 1:TB], [HH[c][1]], [HCAR[c][1]])
                    for c in CH:
                        S.op("act", lambda: nc.scalar.activation(out=UU[c][0][:], in_=GBt[c][0][:], func=AF.Square, scale=0.044715 ** 0.5),
                             reads=[GBt[c][1]], writes=[UU[c][1]])
                    for c in CH:
                        S.op("dve", lambda: nc.vector.scalar_tensor_tensor(out=UU[c][0][:], in0=UU[c][0][:], scalar=1.0, in1=GBt[c][0][:],
                                                                            op0=ALU.add, op1=ALU.mult),
                             reads=[UU[c][1], GBt[c][1]], writes=[UU[c][1]])
                    for c in CH:
                        S.op("act", lambda: nc.scalar.activation(out=UU[c][0][:], in_=UU[c][0][:], func=AF.Sigmoid, scale=1.5957691216057308),
                             reads=[UU[c][1]], writes=[UU[c][1]])
                    for c in CH:
                        S.op("dve", lambda: nc.vector.tensor_tensor(out=UU[c][0][:], in0=UU[c][0][:], in1=GBt[c][0][:], op=ALU.mult),
                             reads=[UU[c][1], GBt[c][1]], writes=[UU[c][1]])
                    for c in CH:
                        S.op("dve", lambda: nc.vector.tensor_tensor(out=OB[c][0][:], in0=UU[c][0][:], in1=HH[c][0][:], op=ALU.mult),
                             reads=[UU[c][1], HH[c][1]], writes=[OB[c][1]])
                    for c in CH:
                        S.dma("sp", oT_d[256 + c * 128:256 + (c + 1) * 128, tcs], OB[c][0][:, :], OB[c][1],
                              reads=[OB[c][1]], writes=[B_oT], waw=False)
                S.barrier()

            with contextlib.ExitStack() as st:
                qr = Ring([(st.enter_context(_sbuf_tensor(f"qTc{i}", [80, S_LEN], BF16)), Buf(f"qTc{i}")) for i in range(2)])
                kr = Ring([(st.enter_context(_sbuf_tensor(f"kTc{i}", [80, S_LEN], BF16)), Buf(f"kTc{i}")) for i in range(2)])
                for kt_, bk_ in kr.items:
                    S.dma("sp", kt_[64:80, :], oh16_d, bk_, writes=[bk_], waw=False)
                vr = Ring([(st.enter_context(_sbuf_tensor(f"Vc{i}", [128, NT, 65], BF16)), Buf(f"Vc{i}")) for i in range(2)])
                for v_, bv_ in vr.items:
                    S.op("pool", lambda v_=v_: nc.gpsimd.memset(v_[:, :, 64:65], 1.0), writes=[bv_], waw=False)
                stripr = Ring([(st.enter_context(_sbuf_tensor(f"strip{i}", [128, STRIP_W], F32)), Buf(f"strip{i}")) for i in range(2)])
                b31 = st.enter_context(_sbuf_tensor("b31", [128, 8], F32))
                B_b31 = Buf("b31")
                S.dma("sp", b31[:], b31_d, B_b31, writes=[B_b31])
                gmask = st.enter_context(_sbuf_tensor("gmask", [128, 2, 512], F32))
                B_gmk = Buf("gmask")
                S.dma("sp", gmask[:], gmask_d, B_gmk, writes=[B_gmk])
                kms = st.enter_context(_sbuf_tensor("kms", [64, 16], F32))
                kmb = st.enter_context(_sbuf_tensor("kmb", [64, 16], BF16))
                B_km = Buf("km")
                gm = st.enter_context(_sbuf_tensor("gm", [128, 512], F32))
                B_gm = Buf("gm")
                m8a = st.enter_context(_sbuf_tensor("m8a", [128, NT, 8], F32))
                B_m8 = Buf("m8a")
                ltt = st.enter_context(_sbuf_tensor("ltt", [128, 512], F32))
                B_lt = Buf("ltt")
                negpad = st.enter_context(_sbuf_tensor("negpad", [128, NT, 80], BF16))
                B_np = Buf("negpad")
                S.op("pool", lambda: nc.gpsimd.memset(negpad[:], 0.0), writes=[B_np])
                tmr = Ring([(st.enter_context(_sbuf_tensor(f"tmpc{i}", [128, 512], F32)), Buf(f"tmpc{i}")) for i in range(5)])
                pTr = Ring([(st.enter_context(_sbuf_tensor(f"pTc{i}", [128, 512], BF16)), Buf(f"pTc{i}")) for i in range(6)])
                numr = Ring([(st.enter_context(_sbuf_tensor(f"numc{i}", [65, 512], F32)), Buf(f"numc{i}")) for i in range(3)])
                rdr = Ring([(st.enter_context(_sbuf_tensor(f"rdc{i}", [65, 512], F32)), Buf(f"rdc{i}")) for i in range(3)])
                ocr = Ring([(st.enter_context(_sbuf_tensor(f"ocs{i}", [64, S_LEN], BF16)), Buf(f"ocs{i}")) for i in range(2)])
                LA = 3
                print(f"[kernel] PC phase l={l}: sbuf bytes remaining {nc.sbuf_bytes_remaining}")
                pump_c, drain_c = (None, None)
                if l + 1 < n_layers:
                    pump_c, drain_c = make_pump(st, next_tiles, B_wbf[l + 1], ["pool"], "pool", nbuf=2)

                wq_c = WQ([(w_in_bf[l], base + h_ * 64, 64, Bw) for h_ in range(8) for base in (QC, KC, VC)], 2)

                def prologue(h):
                    qT, bq = qr.next()
                    kT, bk = kr.next()
                    Vc, bV = vr.next()
                    strip, bstrip = stripr.next()
                    ctx = dict(qT=qT, bq=bq, kT=kT, bk=bk, Vc=Vc, bV=bV, strip=strip, bstrip=bstrip)
                    S.dma("sp", strip[:], strip_d[h], bstrip, writes=[bstrip])
                    for (col, dstT, bdst) in ((QC + h * 64, qT, bq), (KC + h * 64, kT, bk)):
                        wt, bw = wq_c.get()
                        for tc in range(NTC):
                            pb, bpb = pring.next()
                            for dc in range(NDC):
                                mm(pb[0:64, :], wt[:, dc, 0:64], hT[:, dc, tc * 512:(tc + 1) * 512], dc == 0, dc == NDC - 1,
                                   [bw, B_hT], [bpb])
                            copy(evac_eng(), dstT[0:64, tc * 512:(tc + 1) * 512], pb[0:64, :], [bpb], [bdst], waw=False)
                            yield ctx
                    wv, bwv = wq_c.get()
                    for t0 in range(0, NT, 8):
                        pb, bpb = pring.next()
                        for i in range(8):
                            t = t0 + i
                            for dc in range(NDC):
                                mm(pb[:, i * 64:(i + 1) * 64], hT[:, dc, t * 128:(t + 1) * 128], wv[:, dc, 0:64],
                                   dc == 0, dc == NDC - 1, [bwv, B_hT], [bpb])
                        copy(evac_eng(), Vc[:, t0:t0 + 8, 0:64], pb[:, :].rearrange("p (k e) -> p k e", k=8), [bpb], [bV], waw=False)
                        yield ctx
                    for q4 in range(4):
                        S.op("dve", lambda: nc.vector.tensor_reduce(
                            out=kms[:, q4 * 4:(q4 + 1) * 4], in_=kT[0:64, q4 * 1024:(q4 + 1) * 1024].rearrange("p (n k) -> p n k", k=256),
                            axis=AX.X, op=ALU.add), reads=[bk], writes=[B_km], waw=(q4 == 0))
                        yield ctx
                    S.op("dve", lambda: nc.vector.tensor_scalar(out=kmb[:, :], in0=kms[:, :], scalar1=1.0 / 256.0, scalar2=None,
                                                                 op0=ALU.mult),
                         reads=[B_km], writes=[B_km])
                    yield ctx
                    pg, bpg = pring.next()
                    for qt in range(NT):
                        mm(pg[:, qt * 16:(qt + 1) * 16], qT[0:64, qt * 128:(qt + 1) * 128], kmb[:, :], True, True, [bq, B_km], [bpg])
                    yield ctx
                    S.op("dve", lambda: nc.vector.tensor_tensor(out=gm[:, :], in0=pg[:, :], in1=gmask[:, 0, :], op=ALU.add),
                         reads=[bpg, B_gmk], writes=[B_gm])
                    yield ctx
                    for qt in range(NT):
                        S.op("dve", lambda qt=qt: nc.vector.max(out=m8a[:, qt, :], in_=gm[:, qt * 16:(qt + 1) * 16]),
                             reads=[B_gm], writes=[B_m8], waw=(qt == 0))
                        if qt % 4 == 3:
                            yield ctx
                    S.op("dve", lambda: nc.vector.tensor_tensor(out=ltt[:, :].rearrange("p (q n) -> p q n", n=16),
                                                                 in0=gm[:, :].rearrange("p (q n) -> p q n", n=16),
                                                                 in1=m8a[:, :, 2:3].to_broadcast([128, NT, 16]), op=ALU.is_lt),
                         reads=[B_gm, B_m8], writes=[B_lt])
                    yield ctx
                    S.op("dve", lambda: nc.vector.tensor_tensor(out=negpad[:, :, 64:80],
                                                                 in0=ltt[:, :].rearrange("p (q n) -> p q n", n=16),
                                                                 in1=gmask[:, 1, :].rearrange("p (q n) -> p q n", n=16), op=ALU.mult),
                         reads=[B_lt, B_gmk], writes=[B_np])
                    yield ctx
                    for g4 in range(NT // 4):
                        pt, bpt = sring.next()
                        for i in range(4):
                            qt = g4 * 4 + i
                            mm(pt[0:80, i * 128:(i + 1) * 128], negpad[:, qt, :], ident[:, :], True, True, [B_np, B_ident], [bpt])
                        copy(evac_eng(), qT[64:80, g4 * 512:(g4 + 1) * 512], pt[64:80, :], [bpt], [bq], waw=False)
                        yield ctx

                def run_all(gen):
                    ctx = None
                    for ctx in gen:
                        pass
                    return ctx

                cur = run_all(prologue(0))
                for h in range(8):
                    qT, bq, kT, bk, Vc, bV = cur["qT"], cur["bq"], cur["kT"], cur["bk"], cur["Vc"], cur["bV"]
                    strip, bstrip = cur["strip"], cur["bstrip"]
                    nxt_gen = prologue(h + 1) if h + 1 < 8 else None
                    nxt = None
                    ocs, bocs = ocr.next()
                    tiles = [(qc, kt) for qc in range(NTC) for kt in range(4 * qc + 4)]
                    accs = {}
                    inflight = []

                    def stage1(qc, kt):
                        j = kt - 4 * qc
                        n0 = max(0, j) * 128
                        off = min(qc * 512 - kt * 128 + 384, STRIP_OFFMAX)
                        ps, bps = sring.next()
                        mm(ps[:, n0:512], kT[0:80, kt * 128:(kt + 1) * 128], qT[0:80, qc * 512 + n0:(qc + 1) * 512],
                           True, True, [bk, bq], [bps])
                        pT, bpT = pTr.next()
                        if qc * 512 - kt * 128 >= 1664:
                            S.op("act", lambda: nc.scalar.activation(out=pT[:, :], in_=ps[:, :], func=AF.Exp,
                                                                      bias=b31[:, h:h + 1], scale=SCALE),
                                 reads=[bps, B_b31], writes=[bpT])
                        else:
                            tm, btm = tmr.next()
                            S.op("dve", lambda: nc.vector.scalar_tensor_tensor(
                                out=tm[:, n0:512], in0=ps[:, n0:512], scalar=SCALE, in1=strip[:, off + n0:off + 512],
                                op0=ALU.mult, op1=ALU.add), reads=[bps, bstrip], writes=[btm])
                            S.op("act", lambda: nc.scalar.activation(out=pT[:, n0:512], in_=tm[:, n0:512], func=AF.Exp),
                                 reads=[btm], writes=[bpT])
                        return (qc, kt, n0, pT, bpT)

                    def stage2(qc, kt, n0, pT, bpT):
                        nkt = 4 * qc + 4
                        if kt == 0:
                            accs[qc] = aring.next()
                        pa, bpa = accs[qc]
                        mm(pa[0:65, n0:512], Vc[:, kt, :], pT[:, n0:512], kt == 0, kt == nkt - 1, [bV, bpT], [bpa])
                        if kt == nkt - 1:
                            num, bnum = numr.next()
                            rd, brd = rdr.next()
                            copy("act", num[:, :], pa[0:65, :], [bpa], [bnum])
                            S.op("act", lambda: nc.scalar.activation(out=rd[64:65, :], in_=pa[64:65, :], func=AF.Ln),
                                 reads=[bpa], writes=[brd])
                            S.op("act", lambda: nc.scalar.activation(out=rd[64:65, :], in_=rd[64:65, :], func=AF.Exp, scale=-1.0),
                                 reads=[brd], writes=[brd])

                            def fin_b(qc=qc, num=num, bnum=bnum, rd=rd, brd=brd):
                                pb, bpb = pring.next()
                                mm(pb[0:64, :], ones[64:65, 0:64], rd[64:65, :], True, True, [B_ones, brd], [bpb])
                                S.op("dve", lambda: nc.vector.tensor_tensor(out=ocs[:, qc * 512:(qc + 1) * 512], in0=num[0:64, :],
                                                                             in1=pb[0:64, :], op=ALU.mult),
                                     reads=[bnum, bpb], writes=[bocs], waw=False)
                            deferred.append([10, fin_b])

                    deferred = []
                    for i in range(len(tiles) + LA):
                        if i < len(tiles):
                            inflight.append(stage1(*tiles[i]))
                        if i >= LA:
                            stage2(*inflight.pop(0))
                        for d_ in deferred:
                            d_[0] -= 1
                        while deferred and deferred[0][0] <= 0:
                            deferred.pop(0)[1]()
                        if nxt_gen is not None and i % 3 == 2:
                            try:
                                nxt = next(nxt_gen)
                            except StopIteration:
                                nxt_gen = None
                        if pump_c is not None and i % 7 == 3:
                            pump_c(1)
                    while deferred:
                        deferred.pop(0)[1]()
                    if nxt_gen is not None:
                        r_ = run_all(nxt_gen)
                        nxt = r_ if r_ is not None else nxt
                    S.dma("pool", oT_d[768 + h * 64:768 + (h + 1) * 64, :], ocs[:, :], bocs, reads=[bocs], writes=[B_oT], waw=False)
                    cur = nxt
                if drain_c is not None:
                    drain_c()
                S.barrier()
        if stop_after == "mix":
            return None

        with contextlib.ExitStack() as st:
            R1 = st.enter_context(_sbuf_tensor("R1", [128, NFC * D], BF16))
            B_R1 = Buf("R1")
            Pw_t = st.enter_context(_sbuf_tensor("Pw", [128, 10, D], BF16))
            B_Pw = Buf("Pw")
            Pw = Pw_t[:, :, :]
            S.dma("sp", Pw, pcat_bf[l].rearrange("(k p) m -> p k m", p=128), B_Pw, reads=[Bw], writes=[B_Pw])
            Wo_t = st.enter_context(_sbuf_tensor("Wo", [128, NDC, D], BF16))
            B_Wo = Buf("Wo")
            Wo = Wo_t[:, :, :]
            S.dma("sp", Wo, w_out_bf[l].rearrange("(k p) m -> p k m", p=128), B_Wo, reads=[Bw], writes=[B_Wo])
            Wd = R1[:, :].rearrange("p (k m) -> p k m", k=NFC)
            wd_src = w_down_bf[l].rearrange("(k p) m -> p k m", p=128)
            S.dma("sp", Wd[:, 0:11, :], wd_src[:, 0:11, :], B_R1, reads=[Bw], writes=[B_R1])
            S.dma("sp", Wd[:, 11:22, :], wd_src[:, 11:22, :], B_R1, reads=[Bw], writes=[B_R1], waw=False)
            oTr = Ring([(st.enter_context(_sbuf_tensor("oTt", [128, 10, 512], BF16)), Buf("oTt"))])
            mT = st.enter_context(_sbuf_tensor("mT", [128, NDC, 512], BF16))
            B_mT = Buf("mT")
            gtr = Ring([(st.enter_context(_sbuf_tensor(f"gTt{i}", [128, 3, 512], BF16)), Buf(f"gTt{i}")) for i in range(3)])
            mtr = Ring([(st.enter_context(_sbuf_tensor(f"mtmp{i}", [128, 512], F32)), Buf(f"mtmp{i}")) for i in range(4)])
            xr = Ring([(st.enter_context(_sbuf_tensor(f"fx{i}", [128, D], F32)), Buf(f"fx{i}")) for i in range(2)])
            x1b = st.enter_context(_sbuf_tensor("x1b", [128, 4, D], F32))
            B_x1 = [Buf(f"x1b{i}") for i in range(4)]
            nr = Ring([(st.enter_context(_sbuf_tensor(f"fn{i}", [128, D], BF16)), Buf(f"fn{i}")) for i in range(4)])
            sr = Ring([(st.enter_context(_sbuf_tensor(f"fs{i}", [128, 2], F32)), Buf(f"fs{i}")) for i in range(4)])
            h2T = st.enter_context(_sbuf_tensor("h2T", [128, NDC, 512], BF16))
            B_h2 = Buf("h2T")
            actT = st.enter_context(_sbuf_tensor("actT", [128, NFC, 512], BF16))
            B_act = Buf("actT")
            sgr = Ring([(st.enter_context(_sbuf_tensor(f"sg{i}", [128, 512], F32)), Buf(f"sg{i}")) for i in range(2)])
            x2r = Ring([(st.enter_context(_sbuf_tensor(f"x2s{i}", [128, D], F32)), Buf(f"x2s{i}")) for i in range(2)])
            gfin = None
            if last:
                gfin = st.enter_context(_sbuf_tensor("gfin", [128, D], F32))
                B_gf = Buf("gfin")
                S.dma("sp", gfin[:], gfin_d, B_gf, writes=[B_gf])
            pump_f, drain_f = (None, None)
            print(f"[kernel] FFN phase l={l}: sbuf bytes remaining {nc.sbuf_bytes_remaining}")
            wq_f = WQ([(w_gu_bf[l], off_ + fc_ * 128, 128, Bw) for _tc in range(NTC) for fc_ in range(NFC) for off_ in (0, FFN)], 4)
            def load_oT(tc):
                cs = slice(tc * 512, (tc + 1) * 512)
                oTt, boT = oTr.next()
                S.dma("sp", oTt[:], oT_d[:, cs].rearrange("(k p) t -> p k t", p=128), boT, reads=[B_oT], writes=[boT])
                return oTt, boT

            def load_gT(tc, dc):
                cs = slice(tc * 512, (tc + 1) * 512)
                gt, bgt = gtr.next()
                S.dma("sp", gt[:], gT_d[:, cs].rearrange("(b c p) t -> c p b t", b=3, p=128)[dc], bgt, reads=[B_gT], writes=[bgt])
                return gt, bgt

            def merge(tc, oT_pre):
                oTt, boT = oT_pre
                gq = [load_gT(tc, 0), load_gT(tc, 1)]
                for dc in range(NDC):
                    if dc + 2 < NDC:
                        gq.append(load_gT(tc, dc + 2))
                    gt, bgt = gq.pop(0)
                    tmps = []
                    for bi, (k0, k1) in enumerate(((0, 2), (2, 6), (6, 10))):
                        pb, bpb = allring.next()
                        for k in range(k0, k1):
                            mm(pb[:, :], Pw[:, k, dc * 128:(dc + 1) * 128], oTt[:, k, :], k == k0, k == k1 - 1, [B_Pw, boT], [bpb])
                        tm, btm = mtr.next()
                        S.op("dve", lambda: nc.vector.tensor_tensor(out=tm[:], in0=pb[:, :], in1=gt[:, bi, :], op=ALU.mult),
                             reads=[bpb, bgt], writes=[btm])
                        tmps.append((tm, btm))
                    S.op("pool", lambda: nc.gpsimd.tensor_tensor(out=tmps[0][0][:], in0=tmps[0][0][:], in1=tmps[1][0][:], op=ALU.add),
                         reads=[tmps[0][1], tmps[1][1]], writes=[tmps[0][1]])
                    S.op("pool", lambda: nc.gpsimd.tensor_tensor(out=mT[:, dc, :], in0=tmps[0][0][:], in1=tmps[2][0][:], op=ALU.add),
                         reads=[tmps[0][1], tmps[2][1]], writes=[B_mT], waw=False)

            merge(0, load_oT(0))
            for tc in range(NTC):
                cs = slice(tc * 512, (tc + 1) * 512)
                for i in range(4):
                    t = tc * 4 + i
                    xt, bxt = xr.next()
                    S.dma("sp", xt[:], x_src[t * 128:(t + 1) * 128, :], bxt, reads=[B_xsrc], writes=[bxt])
                    for half in range(2):
                        pb, bpb = allring.next()
                        for dc in range(NDC):
                            mm(pb[:, :], mT[:, dc, i * 128:(i + 1) * 128], Wo[:, dc, half * 512:(half + 1) * 512],
                               dc == 0, dc == NDC - 1, [B_mT, B_Wo], [bpb])
                        S.op("dve", lambda: nc.vector.tensor_tensor(out=x1b[:, i, half * 512:(half + 1) * 512], in0=pb[:, :],
                                                                     in1=xt[:, half * 512:(half + 1) * 512], op=ALU.add),
                             reads=[bpb, bxt], writes=[B_x1[i]], waw=False)
                oT_next = load_oT(tc + 1) if tc + 1 < NTC else None
                nps = []
                for i in range(4):
                    xn, bxn = nr.next()
                    sm, bsm = sr.next()
                    norm_a(x1b[:, i, :], B_x1[i], xn, bxn, sm, bsm)
                    nps.append((xn, bxn, sm, bsm))
                for i in range(4):
                    norm_b(x1b[:, i, :], B_x1[i], nps[i][0], nps[i][1], nps[i][2], nps[i][3])
                if oT_next is not None:
                    merge(tc + 1, oT_next)
                for i in range(4):
                    transpose_part(nps[i][0], nps[i][1], h2T, i * 128, B_h2, 8, l)
                for fc in range(NFC):
                    if pump_f is not None:
                        pump_f(1)
                    wg, bwg = wq_f.get()
                    wu, bwu = wq_f.get()
                    pg, bpg = allring.next()
                    for dc in range(NDC):
                        mm(pg[:, :], wg[:, dc, :], h2T[:, dc, :], dc == 0, dc == NDC - 1, [bwg, B_h2], [bpg])
                    pu, bpu = allring.next()
                    for dc in range(NDC):
                        mm(pu[:, :], wu[:, dc, :], h2T[:, dc, :], dc == 0, dc == NDC - 1, [bwu, B_h2], [bpu])
                    sg, bsg = sgr.next()
                    S.op("act", lambda: nc.scalar.activation(out=sg[:], in_=pg[:, :], func=AF.Silu), reads=[bpg], writes=[bsg])
                    S.op("dve", lambda: nc.vector.tensor_tensor(out=actT[:, fc, :], in0=sg[:], in1=pu[:, :], op=ALU.mult),
                         reads=[bsg, bpu], writes=[B_act], waw=False)
                for i in range(4):
                    t = tc * 4 + i
                    x2, bx2 = x2r.next()
                    for half in range(2):
                        pb, bpb = allring.next()
                        for fc in range(NFC):
                            mm(pb[:, :], actT[:, fc, i * 128:(i + 1) * 128], Wd[:, fc, half * 512:(half + 1) * 512],
                               fc == 0, fc == NFC - 1, [B_act, B_R1], [bpb])
                        S.op("dve", lambda: nc.vector.tensor_tensor(out=x2[:, half * 512:(half + 1) * 512], in0=pb[:, :],
                                                                     in1=x1b[:, i, half * 512:(half + 1) * 512], op=ALU.add),
                             reads=[bpb, B_x1[i]], writes=[bx2], waw=False)
                    if not last:
                        S.dma("pool", x_dst[t * 128:(t + 1) * 128, :], x2[:, :], bx2, reads=[bx2], writes=[B_xdst], waw=False)
                    else:
                        xn, bxn = nr.next()
                        sm, bsm = sr.next()
                        ssq, rstd = sm[:, 0:1], sm[:, 1:2]
                        S.op("act", lambda: nc.scalar.activation(out=xn[:, :], in_=x2[:, :], func=AF.Square, accum_out=ssq),
                             reads=[bx2], writes=[bxn, bsm])
                        S.op("act", lambda: nc.scalar.activation(out=rstd, in_=ssq, func=AF.Sqrt, bias=epsT[:, 0:1], scale=1.0 / D),
                             reads=[bsm, B_vecs], writes=[bsm])
                        S.op("dve", lambda: nc.vector.reciprocal(out=rstd, in_=rstd), reads=[bsm], writes=[bsm])
                        S.op("dve", lambda: nc.vector.scalar_tensor_tensor(out=x2[:, :], in0=x2[:, :], scalar=rstd, in1=gfin[:, :],
                                                                            op0=ALU.mult, op1=ALU.mult),
                             reads=[bx2, bsm, B_gf], writes=[bx2])
                        S.dma("pool", out_d[t * 128:(t + 1) * 128, :], x2[:, :], bx2, reads=[bx2], writes=[B_out], waw=False)
            if drain_f is not None:
                drain_f()
            S.barrier()

    srcs = [(x_in, B_xin), (xs[0], B_xs[0]), (xs[1], B_xs[1])]
    for l in range(n_layers):
        x_src, B_src = srcs[l]
        x_dst, B_dst = srcs[l + 1]
        layer(l, x_src, B_src, x_dst, B_dst, last=(l == n_layers - 1) and not debug)
    S.barrier()
    print(f"[kernel] built program: {S.ninstr} instructions, {S.nsem} semaphores")
    return nc


def _rel_bucket(dist):
    import math
    d = np.maximum(dist, 0)
    df = np.maximum(d, 1).astype(np.float32)
    large = 16 + (np.log(df / np.float32(16)) / np.float32(math.log(2048 / 16)) * np.float32(16)).astype(np.int32)
    large = np.minimum(large, 31)
    return np.where(d < 16, d, large)


def _host_layout(inputs):
    f32 = np.float32
    rel = np.asarray(inputs["rel_bias"], f32)
    ext = np.concatenate([rel, np.full((1, 20), NEG_BIAS, f32)], 0)
    k = np.arange(128)[:, None]
    n = np.arange(256)[None, :]
    delta = n - k
    valid = (delta >= 0) & (delta <= 128)
    biasA = np.empty((128, 12, 256), f32)
    for hd in range(12):
        dil = (1, 4, 16)[hd // 4]
        idx = np.where(valid, _rel_bucket(delta * dil), 32)
        biasA[:, hd, :] = ext[idx, hd]
    c = np.arange(STRIP_W)[None, :]
    dl = c - k - 384
    idx = np.where(dl >= 0, _rel_bucket(dl), 32)
    strip = np.empty((8, 128, STRIP_W), f32)
    for h in range(8):
        strip[h] = ext[idx, 12 + h]
    oh16 = np.zeros((16, S_LEN), ml_dtypes.bfloat16)
    for j in range(16):
        oh16[j, j * 256:(j + 1) * 256] = 1.0
    gmask = np.zeros((128, 2, NT, 16), f32)
    for qt in range(NT):
        gmask[:, 0, qt, qt // 2:] = -1e30
        gmask[:, 1, qt, :qt // 2] = NEGM
    gmask = gmask.reshape(128, 2, 512)
    b31c = np.ascontiguousarray(np.broadcast_to(rel[31:32, 12:20], (128, 8)))
    g_mix = np.asarray(inputs["g_mix"], f32)
    g_ffn = np.asarray(inputs["g_ffn"], f32)
    vecs = np.empty((128, DEPTH, 16), f32)
    vecs[:, :, 0:8] = g_mix.reshape(DEPTH, 8, 128).transpose(2, 0, 1)
    vecs[:, :, 8:16] = g_ffn.reshape(DEPTH, 8, 128).transpose(2, 0, 1)
    lruv = np.empty((128, DEPTH, 4, 8), f32)
    cw = np.asarray(inputs["conv_w"], f32)
    for i in range(4):
        lruv[:, :, :, i] = cw[:, i].reshape(DEPTH, 4, 128).transpose(2, 0, 1)
    for j, key in enumerate(("conv_b", "lru_ba", "lru_bx", "lru_lam")):
        lruv[:, :, :, 4 + j] = np.asarray(inputs[key], f32).reshape(DEPTH, 4, 128).transpose(2, 0, 1)
    lrubd = np.zeros((DEPTH, 2, 128, 4, 128), f32)
    for j, key in enumerate(("lru_wa", "lru_wx")):
        w = np.asarray(inputs[key], f32)
        for cch in range(4):
            for nl in range(2):
                lrubd[:, j, nl * 64:(nl + 1) * 64, cch, nl * 64:(nl + 1) * 64] = w[:, 2 * cch + nl]
    gfin = np.ascontiguousarray(np.broadcast_to(np.asarray(inputs["g_final"], f32)[None, :], (128, D)))
    shared = {
        "w_in": np.ascontiguousarray(inputs["w_in"], f32), "p_a": np.ascontiguousarray(inputs["p_a"], f32),
        "p_b": np.ascontiguousarray(inputs["p_b"], f32), "p_c": np.ascontiguousarray(inputs["p_c"], f32),
        "w_out": np.ascontiguousarray(inputs["w_out"], f32), "w_gu": np.ascontiguousarray(inputs["w_gu"], f32),
        "w_down": np.ascontiguousarray(inputs["w_down"], f32),
        "biasA": biasA, "stripC": strip, "oh16": oh16, "gmask": gmask, "b31c": b31c, "vecs": vecs, "lruv": lruv, "lrubd": lrubd, "g_final_b": gfin,
    }
    return shared


_PROGRAM = {}


def kernel(x, rel_bias, g_mix, w_in, conv_w, conv_b, lru_wa, lru_ba, lru_wx, lru_bx, lru_lam,
           p_a, p_b, p_c, w_out, g_ffn, w_gu, w_down, g_final):
    inputs = dict(x=x, rel_bias=rel_bias, g_mix=g_mix, w_in=w_in, conv_w=conv_w, conv_b=conv_b, lru_wa=lru_wa,
                  lru_ba=lru_ba, lru_wx=lru_wx, lru_bx=lru_bx, lru_lam=lru_lam, p_a=p_a, p_b=p_b, p_c=p_c,
                  w_out=w_out, g_ffn=g_ffn, w_gu=w_gu, w_down=w_down, g_final=g_final)
    shared = _host_layout(inputs)
    xf = np.ascontiguousarray(np.asarray(x, np.float32))
    nb = xf.shape[0]
    nc = build_program()
    in_maps = [dict(shared, x=xf[b]) for b in range(nb)]
    res = run_bass_kernel_spmd(nc, in_maps, core_ids=list(range(nb)))
    return np.stack([np.asarray(r["out"], np.float32) for r in res.results], axis=0)
```
